# Optimizing a Trainium2 kernel written in Bass

```python
import math
import jax, jax.numpy as jnp
from jax import lax
import numpy as np

D_MODEL = 2048
BATCH = 8
SEQ = 2048
DEPTH = 1

HEAD_DIM = 128
FOX_HEADS = 8
MOBA_HEADS = 4
MEM_HEADS = 4
FOX_W = FOX_HEADS * HEAD_DIM
MOBA_W = MOBA_HEADS * HEAD_DIM
MEM_W = MEM_HEADS * HEAD_DIM
MIX_W = FOX_W + MOBA_W + MEM_W
IN_W = 3 * FOX_W + FOX_HEADS + 3 * MOBA_W + MEM_W
D_FF = 5632
N_MEM = 256
FOX_Q_BLOCK = 128
MOBA_BLOCK = 256
MOBA_TOPK = 3
MOBA_Q_CHUNK = 32
REL_BUCKETS = 32
REL_MAX_DIST = 128
EPS = 1e-6
NEG = -1e30

kernel_name = "hymba_fox_moba_macaron_layer"


def rmsnorm(x, g):
    xf = x.astype(jnp.float32)
    y = xf * lax.rsqrt(jnp.mean(xf * xf, axis=-1, keepdims=True) + EPS)
    return (y * g.astype(jnp.float32)).astype(x.dtype)


def swiglu(h, w1, w3, w2):
    return (jax.nn.silu(h @ w1) * (h @ w3)) @ w2


def t5_bucket(dist):
    n = jnp.maximum(dist, 0)
    max_exact = REL_BUCKETS // 2
    nf = jnp.maximum(n, 1).astype(jnp.float32)
    large = max_exact + (jnp.log(nf / max_exact) / math.log(REL_MAX_DIST / max_exact)
                         * (REL_BUCKETS - max_exact)).astype(jnp.int32)
    large = jnp.minimum(large, REL_BUCKETS - 1)
    return jnp.where(n < max_exact, n, large)


def fox_attention(q, k, v, logf):
    B, H, S, Dh = q.shape
    c = jnp.cumsum(logf, axis=-1)
    kpos = jnp.arange(S)
    scale = Dh ** -0.5

    def block(i):
        s0 = i * FOX_Q_BLOCK
        qb = lax.dynamic_slice_in_dim(q, s0, FOX_Q_BLOCK, axis=2)
        cb = lax.dynamic_slice_in_dim(c, s0, FOX_Q_BLOCK, axis=2)
        qpos = s0 + jnp.arange(FOX_Q_BLOCK)
        logits = (jnp.einsum('bhqd,bhkd->bhqk', qb, k).astype(jnp.float32) * scale
                  + cb[..., :, None] - c[..., None, :])
        logits = jnp.where(kpos[None, :] <= qpos[:, None], logits, NEG)
        p = jax.nn.softmax(logits, axis=-1).astype(v.dtype)
        return jnp.einsum('bhqk,bhkd->bhqd', p, v)

    out = lax.map(block, jnp.arange(S // FOX_Q_BLOCK))
    return jnp.moveaxis(out, 0, 2).reshape(B, H, S, Dh)


def moba_attention(q, k, v, rel_bias):
    B, H, S, Dh = q.shape
    nb = -(-S // MOBA_BLOCK)
    pad = nb * MOBA_BLOCK - S
    kp = jnp.pad(k, ((0, 0), (0, 0), (0, pad), (0, 0)))
    vp = jnp.pad(v, ((0, 0), (0, 0), (0, pad), (0, 0)))
    kb = kp.reshape(B, H, nb, MOBA_BLOCK, Dh)
    vb = vp.reshape(B, H, nb, MOBA_BLOCK, Dh)
    kmean = jnp.mean(kb.astype(jnp.float32), axis=3)
    topk = min(MOBA_TOPK, nb)
    scale = Dh ** -0.5
    b_i = jnp.arange(B)[:, None, None, None]
    h_i = jnp.arange(H)[None, :, None, None]
    h_i5 = jnp.arange(H)[None, :, None, None, None]
    blk_ids = jnp.arange(nb)
    offs = jnp.arange(MOBA_BLOCK)

    def chunk(i):
        s0 = i * MOBA_Q_CHUNK
        qc = lax.dynamic_slice_in_dim(q, s0, MOBA_Q_CHUNK, axis=2)
        qpos = s0 + jnp.arange(MOBA_Q_CHUNK)
        own = s0 // MOBA_BLOCK
        gate = jnp.einsum('bhqd,bhnd->bhqn', qc.astype(jnp.float32), kmean)
        gate = jnp.where(blk_ids < own, gate, NEG)
        _, idx = lax.top_k(gate, topk)
        sel_valid = idx < own
        ksel = kb[b_i, h_i, idx]
        vsel = vb[b_i, h_i, idx]
        sel_pos = idx[..., None] * MOBA_BLOCK + offs
        l_sel = jnp.einsum('bhqd,bhqnkd->bhqnk', qc, ksel).astype(jnp.float32) * scale
        l_sel = l_sel + rel_bias[t5_bucket(qpos[:, None, None] - sel_pos), h_i5].astype(jnp.float32)
        l_sel = jnp.where(sel_valid[..., None], l_sel, NEG)
        l_sel = l_sel.reshape(B, H, MOBA_Q_CHUNK, topk * MOBA_BLOCK)
        own_start = own * MOBA_BLOCK
        kown = lax.dynamic_slice_in_dim(kp, own_start, MOBA_BLOCK, axis=2)
        vown = lax.dynamic_slice_in_dim(vp, own_start, MOBA_BLOCK, axis=2)
        dist_own = qpos[:, None] - (own_start + offs)[None, :]
        l_own = jnp.einsum('bhqd,bhkd->bhqk', qc, kown).astype(jnp.float32) * scale
        l_own = l_own + jnp.moveaxis(rel_bias[t5_bucket(dist_own)], -1, 0).astype(jnp.float32)
        l_own = jnp.where(dist_own >= 0, l_own, NEG)
        p = jax.nn.softmax(jnp.concatenate([l_sel, l_own], axis=-1), axis=-1).astype(v.dtype)
        p_sel = p[..., :topk * MOBA_BLOCK].reshape(B, H, MOBA_Q_CHUNK, topk, MOBA_BLOCK)
        p_own = p[..., topk * MOBA_BLOCK:]
        return (jnp.einsum('bhqnk,bhqnkd->bhqd', p_sel, vsel)
                + jnp.einsum('bhqk,bhkd->bhqd', p_own, vown))

    out = lax.map(chunk, jnp.arange(S // MOBA_Q_CHUNK))
    return jnp.moveaxis(out, 0, 2).reshape(B, H, S, Dh)


def memory_attention(q, k, v):
    logits = jnp.einsum('bhqd,bhmd->bhqm', q, k).astype(jnp.float32) * (q.shape[-1] ** -0.5)
    p = jax.nn.softmax(logits, axis=-1).astype(v.dtype)
    return jnp.einsum('bhqm,bhmd->bhqd', p, v)


def setup_inputs(seed: int = 0) -> dict:
    key = jax.random.key(seed)
    ks = jax.random.split(key, 24)
    f32 = jnp.float32

    def nrm(k, shape, fan_in):
        return jax.random.normal(k, shape, f32) * (fan_in ** -0.5)

    def gain(k, shape):
        return 1.0 + 0.05 * jax.random.normal(k, shape, f32)

    L = DEPTH
    return {
        "x": jax.random.normal(ks[0], (BATCH, SEQ, D_MODEL), f32),
        "mem": jax.random.normal(ks[1], (BATCH, N_MEM, D_MODEL), f32),
        "ffn1_norm": gain(ks[2], (L, D_MODEL)),
        "ffn1_w1": nrm(ks[3], (L, D_MODEL, D_FF), D_MODEL),
        "ffn1_w3": nrm(ks[4], (L, D_MODEL, D_FF), D_MODEL),
        "ffn1_w2": nrm(ks[5], (L, D_FF, D_MODEL), D_FF),
        "mix_norm": gain(ks[6], (L, D_MODEL)),
        "mem_norm": gain(ks[7], (L, D_MODEL)),
        "w_in": nrm(ks[8], (L, D_MODEL, IN_W), D_MODEL),
        "b_forget": jax.random.uniform(ks[9], (L, FOX_HEADS), f32, 1.0, 3.0),
        "w_mem_kv": nrm(ks[10], (L, D_MODEL, 2 * MEM_W), D_MODEL),
        "fox_q_gain": gain(ks[11], (L, HEAD_DIM)),
        "fox_k_gain": gain(ks[12], (L, HEAD_DIM)),
        "moba_q_gain": gain(ks[13], (L, HEAD_DIM)),
        "moba_k_gain": gain(ks[14], (L, HEAD_DIM)),
        "mem_q_gain": gain(ks[15], (L, HEAD_DIM)),
        "mem_k_gain": gain(ks[16], (L, HEAD_DIM)),
        "w_out": nrm(ks[17], (L, MIX_W, D_MODEL), MIX_W),
        "ffn2_norm": gain(ks[18], (L, D_MODEL)),
        "ffn2_w1": nrm(ks[19], (L, D_MODEL, D_FF), D_MODEL),
        "ffn2_w3": nrm(ks[20], (L, D_MODEL, D_FF), D_MODEL),
        "ffn2_w2": nrm(ks[21], (L, D_FF, D_MODEL), D_FF),
        "rel_bias": 0.1 * jax.random.normal(ks[22], (REL_BUCKETS, MOBA_HEADS), f32),
    }


def reference(x, mem, ffn1_norm, ffn1_w1, ffn1_w3, ffn1_w2, mix_norm, mem_norm, w_in,
              b_forget, w_mem_kv, fox_q_gain, fox_k_gain, moba_q_gain, moba_k_gain,
              mem_q_gain, mem_k_gain, w_out, ffn2_norm, ffn2_w1, ffn2_w3, ffn2_w2, rel_bias):
    B, S, _ = x.shape
    M = mem.shape[1]
    splits = np.cumsum([FOX_W, FOX_W, FOX_W, FOX_HEADS, MOBA_W, MOBA_W, MOBA_W]).tolist()

    def heads(t, n_heads, length):
        return t.reshape(B, length, n_heads, HEAD_DIM).transpose(0, 2, 1, 3)

    for l in range(DEPTH):
        x = x + 0.5 * swiglu(rmsnorm(x, ffn1_norm[l]), ffn1_w1[l], ffn1_w3[l], ffn1_w2[l])

        h = rmsnorm(x, mix_norm[l])
        proj = h @ w_in[l]
        fq, fk, fv, ff, bq, bk, bv, cq = jnp.split(proj, splits, axis=-1)

        fq = rmsnorm(heads(fq, FOX_HEADS, S), fox_q_gain[l])
        fk = rmsnorm(heads(fk, FOX_HEADS, S), fox_k_gain[l])
        fv = heads(fv, FOX_HEADS, S)
        logf = jax.nn.log_sigmoid(ff.astype(jnp.float32)
                                  + b_forget[l].astype(jnp.float32)).transpose(0, 2, 1)
        o_fox = fox_attention(fq, fk, fv, logf)

        bq = rmsnorm(heads(bq, MOBA_HEADS, S), moba_q_gain[l])
        bk = rmsnorm(heads(bk, MOBA_HEADS, S), moba_k_gain[l])
        bv = heads(bv, MOBA_HEADS, S)
        o_moba = moba_attention(bq, bk, bv, rel_bias)

        cq = rmsnorm(heads(cq, MEM_HEADS, S), mem_q_gain[l])
        mkv = rmsnorm(mem, mem_norm[l]) @ w_mem_kv[l]
        ck, cv = jnp.split(mkv, 2, axis=-1)
        ck = rmsnorm(heads(ck, MEM_HEADS, M), mem_k_gain[l])
        cv = heads(cv, MEM_HEADS, M)
        o_mem = memory_attention(cq, ck, cv)

        o = jnp.concatenate([o_fox, o_moba, o_mem], axis=1)
        o = o.transpose(0, 2, 1, 3).reshape(B, S, MIX_W)
        x = x + o @ w_out[l]

        x = x + 0.5 * swiglu(rmsnorm(x, ffn2_norm[l]), ffn2_w1[l], ffn2_w3[l], ffn2_w2[l])
    return x
```

```python
import math
from contextlib import ExitStack

import numpy as np
import ml_dtypes

import concourse.bass as bass
import concourse.mybir as mybir
from concourse.bass_utils import run_bass_kernel_spmd

F32 = mybir.dt.float32
BF16 = mybir.dt.bfloat16
AF = mybir.ActivationFunctionType
ALU = mybir.AluOpType
AX = mybir.AxisListType

D = 2048
S = 2048
DFF = 5632
NKC = D // 128
NFC = DFF // 128
T = 1024
NT = S // T
INW = 5128
EPS = 1e-6
SCALE = 128 ** -0.5
RS = math.sqrt(128.0)
NEGB = -30000.0
GW = 1152
GV = 1280
MW = 896
C_FQ, C_FK, C_FV, C_FF, C_BQ, C_BK, C_BV, C_CQ = 0, 1024, 2048, 3072, 3080, 3592, 4104, 4616


class Sem:
    def __init__(self, h):
        self.h = h
        self.count = 0


class Prog:
    NAMES = ["pe", "act", "dve", "pool", "sp"]

    def __init__(self, nc, es):
        self.nc, self.es = nc, es
        self.ops = {n: [] for n in self.NAMES}
        self.csem = {n: Sem(es.enter_context(nc.semaphore("c_" + n))) for n in self.NAMES}
        self.dsems = []
        self.pending = {n: [] for n in self.NAMES}
        self.nsem = 0
        self.free_sems = []
        self.scopes = []

    def dsem(self):
        if self.free_sems:
            s = self.free_sems.pop()
        else:
            s = Sem(self.es.enter_context(self.nc.semaphore("d%d" % self.nsem)))
            self.nsem += 1
            self.dsems.append(s)
        if self.scopes:
            self.scopes[-1].append(s)
        return s

    def scope_begin(self):
        self.scopes.append([])

    def scope_end(self):
        self.free_sems.extend(self.scopes.pop())

    def op(self, eng, fn, waits=(), sig=False):
        w = [x for x in waits if x is not None] + self.pending[eng]
        self.pending[eng] = []
        ev = None
        s = self.csem[eng]
        if sig:
            s.count += 1
            ev = (s, s.count)
        self.ops[eng].append((fn, w, 1 if sig else 0, s))
        return ev

    def dma(self, eng, out, in_, sem, waits=()):
        w = [x for x in waits if x is not None] + self.pending[eng]
        self.pending[eng] = []
        sem.count += 16
        self.ops[eng].append((lambda e, o=out, i=in_: e.dma_start(out=o, in_=i), w, 16, sem))
        return (sem, sem.count)

    def fence(self):
        evs = []
        evs.append(self.op("act", lambda e: e.activation(out=self.mka[:, 0:1], in_=self.mka[:, 1:2], func=AF.Copy), waits=[self.ev_mka], sig=True))
        evs.append(self.op("dve", lambda e: e.memset(self.mkd[:, 0:1], 0.0), sig=True))
        evs.append(self.op("pe", lambda e: e.matmul(self.mkp[0:8, 0:8], self.mkb[:, 0:8], self.mkb[:, 0:8], start=True, stop=True),
                           waits=list(self.bank_free[7]) + [self.ev_mkb], sig=True))
        for s in self.dsems:
            if s.count:
                evs.append((s, s.count))
        for n in self.NAMES:
            if n != "pool":
                self.pending[n] = list(evs)
        self.last_fence = list(evs)
        return evs

    def replay(self, block):
        def run(e, name):
            waited = {}
            for fn, waits, inc, sem in self.ops[name]:
                for (s, v) in waits:
                    if waited.get(id(s), 0) < v:
                        e.wait_ge(s.h, v)
                        waited[id(s)] = v
                ins = fn(e)
                if inc:
                    ins.then_inc(sem.h, inc)
            for (s, v) in self.pending[name]:
                if waited.get(id(s), 0) < v:
                    e.wait_ge(s.h, v)
                    waited[id(s)] = v

        @block.tensor
        def _(e):
            run(e, "pe")

        @block.scalar
        def _(e):
            run(e, "act")

        @block.vector
        def _(e):
            run(e, "dve")

        @block.gpsimd
        def _(e):
            run(e, "pool")

        @block.sync
        def _(e):
            run(e, "sp")


class Ring:
    def __init__(self, P, es, name, n, shape, dtype, dma=False):
        self.n = n
        self.t = [P.sb("%s%d" % (name, i), shape, dtype, es) for i in range(n)]
        self.free = [[] for _ in range(n)]
        self.sem = [P.dsem() for _ in range(n)] if dma else None
        self.i = 0

    def next(self):
        k = self.i % self.n
        self.i += 1
        return k, self.t[k]


def t5_bucket_np(d):
    d = np.asarray(d)
    n = np.maximum(d, 0)
    nf = np.maximum(n, 1).astype(np.float32)
    large = 16 + (np.log(nf / np.float32(16)) / np.float32(math.log(128 / 16)) * np.float32(16)).astype(np.int32)
    large = np.minimum(large, 31)
    return np.where(n < 16, n, large)


def host_consts():
    bf = ml_dtypes.bfloat16
    ident = np.eye(128, dtype=np.float32)
    anti = ident[::-1].copy()
    onesdiv = np.full((128, 128), 1.0 / 128, np.float32)
    ones = np.ones((128, 128), np.float32)
    k = np.arange(128)[:, None]
    u = np.arange(MW)[None, :]
    mstrip = np.where(k + u - 511 >= 0, 0.0, NEGB).astype(np.float32)
    ones3z = np.zeros((128, 128), np.float32)
    ones3z[0:3, :] = 1.0
    cb16 = np.concatenate([ident, anti, onesdiv, ones, ones3z, mstrip], axis=1).astype(bf)
    pm = np.zeros((128, 2, 4, 8), np.float32)
    for n in range(2):
        for g in range(4):
            own = 4 + 2 * n + g // 2
            pm[:, n, g, own:] = -1e30
    cf32 = np.concatenate([ident, pm.reshape(128, 64)], axis=1).astype(np.float32)
    esel = np.zeros((128, 8, 128), np.float32)
    for j in range(8):
        esel[j, j, :] = 1.0
    esel = esel.reshape(128, 1024).astype(bf)
    i = np.arange(GV)
    dd = i - 511
    bk = t5_bucket_np(dd)
    oh = np.zeros((32, GV), np.float32)
    valid = dd >= 0
    oh[bk[valid], i[valid]] = 1.0
    negv = np.where(valid, 0.0, NEGB).astype(np.float32)[None, :].repeat(4, axis=0)
    return cb16, cf32, esel, oh, negv


def build(debug=False, stop_after=None):
    nc = bass.Bass("TRN2", target_bir_lowering=False)
    ikind = "ExternalOutput" if debug else "Internal"

    def din(name, shape, dt=F32):
        return nc.dram_tensor(name, list(shape), dt, kind="ExternalInput").ap()

    def dscr(name, shape, dt):
        return nc.dram_tensor(name, list(shape), dt, kind=ikind).ap()

    x = din("x", [S, D])
    mem = din("mem", [256, D])
    w_ffn = [(din("ffn1_w1", [D, DFF]), din("ffn1_w3", [D, DFF]), din("ffn1_w2", [DFF, D])),
             (din("ffn2_w1", [D, DFF]), din("ffn2_w3", [D, DFF]), din("ffn2_w2", [DFF, D]))]
    w_in = din("w_in", [D, INW])
    w_mkv = din("w_mem_kv", [D, 1024])
    w_out = din("w_out", [D, D])
    normsT = din("normsT", [128, 64])
    gains = din("gains", [128, 6])
    bfg = din("b_forget", [8, 1])
    relb = din("rel_bias", [32, 4])
    cb16_d = din("cb16", [128, 640 + MW], BF16)
    cf32_d = din("cf32", [128, 192])
    esel_d = din("esel", [128, 1024], BF16)
    oh_d = din("oh", [32, GV])
    negv_d = din("negv", [4, GV])
    out = nc.dram_tensor("out", [S, D], F32, kind="ExternalOutput").ap()

    x1s = dscr("x1s", [S, D], F32)
    x2s = dscr("x2s", [S, D], F32)
    QTs = dscr("QTs", [16, 128, S], BF16)
    KTs = dscr("KTs", [12, 128, S], BF16)
    Vs = dscr("Vs", [S, 1536], BF16)
    OTs = dscr("OTs", [16, 128, S], BF16)
    MRs = dscr("MRs", [4, 8, 1024], BF16)
    cs3 = dscr("cs3", [3, 8, S], BF16)
    gv2d = dscr("gv2d", [4, GV], BF16)
    HTd = dscr("HTd", [128, NKC, T], BF16)

    with ExitStack() as es:
        P = Prog(nc, es)
        _uid = [0]

        def sb(name, shape, dt, st=es):
            _uid[0] += 1
            return st.enter_context(nc.sbuf_tensor("s%d_%s" % (_uid[0], name), list(shape), dt))
        P.sb = sb
        P.mka = sb("mka", [128, 2], F32)
        P.mkd = sb("mkd", [128, 1], F32)
        P.mkb = sb("mkb", [128, 8], BF16)
        cb = sb("cb", [128, 640 + MW], BF16)
        cf = sb("cf", [128, 192], F32)
        esel = sb("esel", [128, 1024], BF16)
        gT = sb("gT", [128, 64], F32)
        gcol = sb("gcol", [128, 6], F32)
        epsT = sb("epsT", [128, 1], F32)
        oneT = sb("oneT", [128, 1], F32)
        negb = sb("negb", [8, 1], F32)
        bft = sb("bft", [8, 1], F32)
        CK = sb("CK", [128, 4 * 256], BF16)
        CV = sb("CV", [128, 2 * 512], BF16)
        SPT = sb("SPT", [8, S], F32)
        kmT = sb("kmT", [128, 32], F32)
        CpC = sb("CpC", [128, 128], F32)
        stat = sb("stat", [128, 12], F32)
        stat2 = sb("stat2", [128, 12], F32)
        HT = sb("HT", [128, NKC, T], BF16)
        identb = cb[:, 0:128]
        antib = cb[:, 128:256]
        onesdiv = cb[:, 256:384]
        onesb = cb[:, 384:512]
        ones3z = cb[:, 512:640]
        mstrip = cb[:, 640:640 + MW]
        identf = cf[:, 0:128]
        pastm = cf[:, 128:192]
        WR = Ring(P, es, "wr", 4, [128, 4096], BF16, dma=True)
        ps = [es.enter_context(nc.psum_tensor("ps%d" % i, [128, 512], F32)) for i in range(8)]
        psb = [p.bitcast(BF16) for p in ps]
        P.mkp = ps[7]
        bank_free = [[] for _ in range(8)]
        P.bank_free = bank_free
        P.last_fence = []

        s_setup = P.dsem()
        ld = []
        for (o, i) in [(cb[:], cb16_d), (cf[:], cf32_d), (esel[:], esel_d), (gcol[:], gains), (gT[:], normsT), (bft[:], bfg)]:
            ld.append(P.dma("sp", o, i, s_setup))
        ev_setup = ld[-1]
        P.op("dve", lambda e: e.memset(epsT[:], EPS))
        P.op("dve", lambda e: e.memset(oneT[:], 1.0))
        P.ev_mka = P.op("dve", lambda e: e.memset(P.mka[:], 0.0), sig=True)
        P.ev_mkb = P.op("dve", lambda e: e.memset(P.mkb[:], 0.0), sig=True)
        P.op("dve", lambda e: e.memset(kmT[:], 0.0))
        P.op("dve", lambda e: e.tensor_scalar(out=negb[:], in0=bft[:], scalar1=-1.0, scalar2=None, op0=ALU.mult),
             waits=[ev_setup])
        P.fence()

        with ExitStack() as ph:
            P.scope_begin()
            oh = sb("oh", [32, GV], F32, ph)
            negv = sb("negv", [4, GV], F32, ph)
            rbt = sb("rbt", [32, 4], F32, ph)
            gvs = sb("gvs", [4, GV], BF16, ph)
            s1 = P.dsem()
            P.dma("sp", oh[:], oh_d, s1)
            P.dma("sp", negv[:], negv_d, s1)
            ev = P.dma("sp", rbt[:], relb, s1)
            evs = []
            for c in range(0, GV, 512):
                w = min(512, GV - c)
                b = c // 512
                e1 = P.op("pe", lambda e, b=b, c=c, w=w: e.matmul(ps[b][0:4, 0:w], rbt[:, 0:4], oh[:, c:c + w], start=True, stop=True),
                          waits=[ev], sig=True)
                evs.append(P.op("dve", lambda e, b=b, c=c, w=w: e.scalar_tensor_tensor(
                    out=gvs[:, c:c + w], in0=ps[b][0:4, 0:w], scalar=RS, in1=negv[:, c:c + w], op0=ALU.mult, op1=ALU.add),
                    waits=[e1], sig=True))
            ev = P.dma("sp", gv2d, gvs[:], s1, waits=evs)
            P.fence()
            P.scope_end()

        ht_ready = [[]]

        def normpass(src_rows, norm_idx, ntok, col_base=0, extra_waits=()):
            with ExitStack() as ph:
                P.scope_begin()
                XT = Ring(P, ph, "xt", 3, [128, D], F32, dma=True)
                XN = Ring(P, ph, "xn", 3, [128, D], BF16)
                cp_evs = []
                tq = 0
                H = D // 2
                for g in range(ntok // 128):
                    c = g % 4
                    k, xt = XT.next()
                    ev_ld = P.dma("sp", xt[:], src_rows[g * 128:(g + 1) * 128, :], XT.sem[k],
                                  waits=XT.free[k] + list(extra_waits))
                    kn, xn = XN.next()
                    ev_sq = P.op("act", lambda e, xn=xn, xt=xt, c=c: e.activation(
                        out=xn[:], in_=xt[:], func=AF.Square, accum_out=stat[:, c:c + 1]),
                        waits=[ev_ld] + XN.free[kn], sig=True)
                    ev_ln = P.op("act", lambda e, c=c: e.activation(
                        out=stat[:, 4 + c:5 + c], in_=stat[:, c:c + 1], func=AF.Ln, bias=epsT[:, 0:1], scale=1.0 / D),
                        waits=[ev_sq], sig=True)
                    ev_r = P.op("act", lambda e, c=c: e.activation(
                        out=stat[:, 8 + c:9 + c], in_=stat[:, 4 + c:5 + c], func=AF.Exp, scale=-0.5),
                        waits=[ev_ln], sig=True)
                    ev_a = P.op("act", lambda e, xn=xn, xt=xt, c=c: e.activation(
                        out=xn[:, 0:H], in_=xt[:, 0:H], func=AF.Copy, scale=stat[:, 8 + c:9 + c]),
                        waits=[ev_r], sig=True)
                    ev_d = P.op("dve", lambda e, xn=xn, xt=xt, c=c: e.tensor_scalar(
                        out=xn[:, H:D], in0=xt[:, H:D], scalar1=stat[:, 8 + c:9 + c], scalar2=None, op0=ALU.mult),
                        waits=[ev_r], sig=True)
                    XT.free[k] = [ev_a, ev_d]
                    ev_t = None
                    for q in range(4):
                        b = tq % 4
                        tq += 1
                        for i in range(4):
                            kc = 4 * q + i
                            ev_t = P.op("pe", lambda e, b=b, i=i, kc=kc, xn=xn: e.transpose(
                                psb[b][:, i * 128:(i + 1) * 128], xn[:, kc * 128:(kc + 1) * 128], identb),
                                waits=([ev_a if q < 2 else ev_d] + bank_free[b]) if i == 0 else (), sig=(i == 3))
                        dst = HT[:, 4 * q:4 * q + 4, col_base + g * 128:col_base + (g + 1) * 128]
                        srcv = psb[b][:, 0:512].rearrange("p (a b) -> p a b", a=4)
                        gv = gT[:, norm_idx * 16 + 4 * q:norm_idx * 16 + 4 * q + 4].unsqueeze(2).broadcast_to([128, 4, 128])
                        ev_c = P.op("dve", lambda e, dst=dst, srcv=srcv, gv=gv: e.tensor_tensor(out=dst, in0=srcv, in1=gv, op=ALU.mult),
                                    waits=[ev_t], sig=True)
                        bank_free[b] = [ev_c]
                        cp_evs.append(ev_c)
                    XN.free[kn] = [ev_t]
                ht_ready[0] = cp_evs[-1:]
                P.fence()
                P.scope_end()

        def tokmajor_mm(groups, load_fn, lhs_fn, NK, NG, first_waits=()):
            ev_o = [None] * NG
            for si, (kc0, nk) in enumerate(groups):
                k, wv, ev_w = load_fn(kc0, nk)
                lastgrp = (kc0 + nk == NK)
                if lastgrp:
                    order = [(kk, g) for g in range(NG) for kk in range(nk)]
                else:
                    order = [(kk, g) for kk in range(nk) for g in range(NG)]
                ev = None
                for oi, (kk, g) in enumerate(order):
                    kc = kc0 + kk
                    wl = []
                    if oi == 0:
                        wl = [ev_w] + (list(first_waits) if si == 0 else [])
                    if kc == 0:
                        wl = wl + bank_free[g]
                    islast = oi == len(order) - 1
                    ev = P.op("pe", lambda e, g=g, kc=kc, kk=kk, wv=wv: e.matmul(
                        ps[g][:, :], lhs_fn(kc, g), wv[:, kk, :], start=(kc == 0), stop=(kc == NK - 1)),
                        waits=wl, sig=(kc == NK - 1) or islast)
                    if kc == NK - 1:
                        ev_o[g] = ev
                WR.free[k] = [ev]
            return ev_o

        def w_loader(wview, col0):
            def load_fn(kc0, nk):
                k, wt = WR.next()
                wv = wt[:, 0:nk * 512].rearrange("p (a b) -> p a b", a=nk)
                ev_w = P.dma("pool", wv, wview[:, kc0:kc0 + nk, col0:col0 + 512], WR.sem[k], waits=WR.free[k])
                return k, wv, ev_w
            return load_fn

        def ffn(wset, res_rows, dst_rows, bg=None, ht_next=False):
            w1, w3, w2 = wset
            w1v = w1.rearrange("(kc p) n -> p kc n", p=128)
            w3v = w3.rearrange("(kc p) n -> p kc n", p=128)
            w2v = w2.rearrange("(kc p) n -> p kc n", p=128)
            with ExitStack() as ph:
                P.scope_begin()
                G = sb("G", [128, NFC, T], BF16, ph)
                SIL = Ring(P, ph, "sil", 4, [128, 512], BF16)
                XP = Ring(P, ph, "xp", 14, [128, 512], F32, dma=True)
                pieces = [(s, g) for s in range(4) for g in range(8)]

                def load_piece(i):
                    s, g = pieces[i]
                    k, t = XP.next()
                    return k, t, P.dma("sp", t[:], res_rows[g * 128:(g + 1) * 128, s * 512:(s + 1) * 512], XP.sem[k], waits=XP.free[k])
                xp_q = [load_piece(i) for i in range(8)]

                bg_done = []
                if bg is not None:
                    bsrc, bidx = bg
                    flat = lambda ap: ap.rearrange("p a b -> p (a b)")
                    bxt = [flat(G[:, 28:32, :]).bitcast(F32), flat(G[:, 32:36, :]).bitcast(F32)]
                    bxn = [flat(G[:, 36:38, :]), flat(G[:, 38:40, :])]
                    bst = [flat(G[:, 40:42, :]).rearrange("p (k t) -> p k t", k=16), flat(G[:, 42:44, :]).rearrange("p (k t) -> p k t", k=16)]
                    bsx = [P.dsem(), P.dsem()]
                    bss = [P.dsem(), P.dsem()]
                    bxt_free = [[], []]
                    bxn_free = [[], []]
                    bst_free = [[], []]
                    bstate = {}
                    Hh = D // 2

                    def bgA(g):
                        k = g % 2
                        c = g % 4
                        xt, xn = bxt[k], bxn[k]
                        ev_ld = P.dma("sp", xt, bsrc[g * 128:(g + 1) * 128, :], bsx[k], waits=bxt_free[k])
                        ev_sq = P.op("act", lambda e: e.activation(out=xn, in_=xt, func=AF.Square, accum_out=stat2[:, c:c + 1]),
                                     waits=[ev_ld] + bxn_free[k], sig=True)
                        ev_ln = P.op("act", lambda e: e.activation(out=stat2[:, 4 + c:5 + c], in_=stat2[:, c:c + 1], func=AF.Ln,
                                                                   bias=epsT[:, 0:1], scale=1.0 / D), waits=[ev_sq], sig=True)
                        ev_r = P.op("act", lambda e: e.activation(out=stat2[:, 8 + c:9 + c], in_=stat2[:, 4 + c:5 + c], func=AF.Exp, scale=-0.5),
                                    waits=[ev_ln], sig=True)
                        ev_a = P.op("act", lambda e: e.activation(out=xn[:, 0:Hh], in_=xt[:, 0:Hh], func=AF.Copy, scale=stat2[:, 8 + c:9 + c]),
                                    waits=[ev_r], sig=True)
                        ev_d = P.op("dve", lambda e: e.tensor_scalar(out=xn[:, Hh:D], in0=xt[:, Hh:D], scalar1=stat2[:, 8 + c:9 + c], scalar2=None, op0=ALU.mult),
                                    waits=[ev_r], sig=True)
                        bxt_free[k] = [ev_a, ev_d]
                        bstate[g] = (ev_a, ev_d)

                    def bgB(g):
                        k = g % 2
                        xn, st = bxn[k], bst[k]
                        ev_a, ev_d = bstate.pop(g)
                        ev_t = None
                        ev_c = None
                        for q in range(4):
                            b = 6 + q % 2
                            for i in range(4):
                                kc = 4 * q + i
                                ev_t = P.op("pe", lambda e, b=b, i=i, kc=kc: e.transpose(
                                    psb[b][:, i * 128:(i + 1) * 128], xn[:, kc * 128:(kc + 1) * 128], identb),
                                    waits=([ev_a if q < 2 else ev_d] + bank_free[b]) if i == 0 else (), sig=(i == 3))
                            srcv = psb[b][:, 0:512].rearrange("p (a b) -> p a b", a=4)
                            gv = gT[:, bidx * 16 + 4 * q:bidx * 16 + 4 * q + 4].unsqueeze(2).broadcast_to([128, 4, 128])
                            dst = st[:, 4 * q:4 * q + 4, :]
                            ev_c = P.op("dve", lambda e, dst=dst, srcv=srcv, gv=gv: e.tensor_tensor(out=dst, in0=srcv, in1=gv, op=ALU.mult),
                                        waits=[ev_t] + (bst_free[k] if q == 0 else []), sig=True)
                            bank_free[b] = [ev_c]
                        bxn_free[k] = [ev_t]
                        ev_s = P.dma("sp", HTd[:, :, g * 128:(g + 1) * 128], st, bss[k], waits=[ev_c])
                        bst_free[k] = [ev_s]
                        bg_done.append(ev_s)
                        bg_done.append(ev_t)

                def bg_tick(j):
                    if bg is None:
                        return
                    if j % 3 == 0 and j // 3 < 8:
                        bgA(j // 3)
                    if j % 3 == 2 and j // 3 < 8:
                        bgB(j // 3)

                g_evs = []
                ev_last_s1 = None
                for j in range(NFC):
                    k, wt = WR.next()
                    w1s = wt[:, 0:2048].rearrange("p (a b) -> p a b", a=16)
                    w3s = wt[:, 2048:4096].rearrange("p (a b) -> p a b", a=16)
                    P.dma("pool", w1s, w1v[:, :, j * 128:(j + 1) * 128], WR.sem[k], waits=WR.free[k])
                    ev_w = P.dma("pool", w3s, w3v[:, :, j * 128:(j + 1) * 128], WR.sem[k])
                    first = True
                    for n in range(2):
                        u = 2 * j + n
                        pa, pb = 2 * (u % 3), 2 * (u % 3) + 1
                        ev_mm = {}
                        for wi, ws in enumerate((w1s, w3s)):
                            b = pa if wi == 0 else pb
                            ev = None
                            for kc in range(NKC):
                                wl = []
                                if first:
                                    wl = [ev_w] + (ht_ready[0] if j == 0 else [])
                                    first = False
                                if kc == 0:
                                    wl = wl + bank_free[b]
                                ev = P.op("pe", lambda e, b=b, ws=ws, kc=kc, n=n: e.matmul(
                                    ps[b][:, :], ws[:, kc, :], HT[:, kc, n * 512:(n + 1) * 512],
                                    start=(kc == 0), stop=(kc == NKC - 1)), waits=wl, sig=(kc == NKC - 1))
                            ev_mm[wi] = ev
                        ev_last_s1 = ev_mm[1]
                        ks, st_ = SIL.next()
                        ev_s = P.op("act", lambda e, st_=st_, b=pa: e.activation(out=st_[:], in_=ps[b][:, :], func=AF.Silu),
                                    waits=[ev_mm[0]] + SIL.free[ks], sig=True)
                        bank_free[pa] = [ev_s]
                        ev_g = P.op("dve", lambda e, st_=st_, b=pb, j=j, n=n: e.tensor_tensor(
                            out=G[:, j, n * 512:(n + 1) * 512], in0=st_[:], in1=ps[b][:, :], op=ALU.mult),
                            waits=[ev_s, ev_mm[1]] + (bg_done if (j == 28 and n == 0) else []), sig=True)
                        SIL.free[ks] = [ev_g]
                        bank_free[pb] = [ev_g]
                        g_evs.append(ev_g)
                    WR.free[k] = [ev_last_s1]
                    bg_tick(j)
                if ht_next:
                    s_hn = P.dsem()
                    ht_ready[0] = [P.dma("sp", HT[:, :, :], HTd, s_hn, waits=[ev_last_s1] + bg_done)]
                store_evs = []
                pi = 0
                groups44 = [(kg, min(8, NFC - kg)) for kg in range(0, NFC, 8)]
                for s in range(4):
                    ev_o = tokmajor_mm(groups44, w_loader(w2v, s * 512), lambda kc, g: G[:, kc, g * 128:(g + 1) * 128],
                                       NFC, 8, first_waits=[g_evs[-1]] if s == 0 else ())
                    for g in range(8):
                        kx, xt_, ev_x = xp_q.pop(0)
                        ev_e = P.op("dve", lambda e, xt_=xt_, g=g: e.scalar_tensor_tensor(
                            out=xt_[:], in0=ps[g][:, :], scalar=0.5, in1=xt_[:], op0=ALU.mult, op1=ALU.add),
                            waits=[ev_o[g], ev_x], sig=True)
                        bank_free[g] = [ev_e]
                        ev_st = P.dma("sp", dst_rows[g * 128:(g + 1) * 128, s * 512:(s + 1) * 512], xt_[:], XP.sem[kx], waits=[ev_e])
                        XP.free[kx] = [ev_st]
                        store_evs.append(ev_st)
                        if pi + 8 < len(pieces):
                            xp_q.append(load_piece(pi + 8))
                        pi += 1
                P.fence()
                P.scope_end()
                return store_evs[-4:]

        def headnorm_epilogue(st):
            pass

        class HeadNorm:
            def __init__(self, ph, N):
                self.N = N
                self.SQ = Ring(P, ph, "hsq", 2, [128, N], BF16)
                self.LN = Ring(P, ph, "hln", 2, [128, N], F32)
                self.RB = Ring(P, ph, "hrb", 2, [128, N], F32)
                self.cnt = 0
                self.prev = None

            def main(self, wsl, rhs_fn, gidx, out_fn, first_waits=()):
                N = self.N
                i = self.cnt
                self.cnt += 1
                b = i % 3
                ev = None
                for kc in range(NKC):
                    wl = (list(first_waits) + bank_free[b]) if kc == 0 else []
                    ev = P.op("pe", lambda e, b=b, kc=kc, wsl=wsl, rhs_fn=rhs_fn: e.matmul(
                        ps[b][:, 0:N], wsl[:, kc, :], rhs_fn(kc), start=(kc == 0), stop=(kc == NKC - 1)),
                        waits=wl, sig=(kc == NKC - 1))
                ks, sq = self.SQ.next()
                ev_sq = P.op("act", lambda e, sq=sq, b=b: e.activation(out=sq[:], in_=ps[b][:, 0:N], func=AF.Square),
                             waits=[ev] + self.SQ.free[ks], sig=True)
                cur = (i, b, ks, sq, ev_sq, gidx, out_fn, ev)
                self.flush()
                self.prev = cur
                return ev

            def flush(self):
                if self.prev is None:
                    return
                N = self.N
                i, b, ks, sq, ev_sq, gidx, out_fn, ev_main = self.prev
                self.prev = None
                bq = 3 + i % 2
                ev_on = P.op("pe", lambda e, bq=bq, sq=sq: e.matmul(ps[bq][:, 0:N], onesdiv, sq[:], start=True, stop=True),
                             waits=[ev_sq] + bank_free[bq], sig=True)
                self.SQ.free[ks] = [ev_on]
                kl, ln = self.LN.next()
                ev_ln = P.op("act", lambda e, ln=ln, bq=bq: e.activation(out=ln[:], in_=ps[bq][:, 0:N], func=AF.Ln, bias=epsT[:, 0:1], scale=1.0),
                             waits=[ev_on] + self.LN.free[kl], sig=True)
                bank_free[bq] = [ev_ln]
                kr, rb = self.RB.next()
                ev_r = P.op("act", lambda e, ln=ln, rb=rb: e.activation(out=rb[:], in_=ln[:], func=AF.Exp, scale=-0.5),
                            waits=[ev_ln] + self.RB.free[kr], sig=True)
                self.LN.free[kl] = [ev_r]
                ev_out = out_fn(ps[b][:, 0:N], gcol[:, gidx:gidx + 1], rb, [ev_r, ev_main])
                self.RB.free[kr] = [ev_out]
                bank_free[b] = [ev_out]

        def phase_mem():
            normpass(mem, 3, 256)
            with ExitStack() as ph:
                P.scope_begin()
                HN = HeadNorm(ph, 256)
                wv = w_mkv.rearrange("(kc p) n -> p kc n", p=128)
                for hp in range(2):
                    k, wt = WR.next()
                    ev_w = None
                    for hh in range(2):
                        h = hp * 2 + hh
                        wsl = wt[:, hh * 2048:(hh + 1) * 2048].rearrange("p (a b) -> p a b", a=16)
                        ev_w = P.dma("pool", wsl, wv[:, :, h * 128:(h + 1) * 128], WR.sem[k], waits=WR.free[k] if hh == 0 else ())
                    evm = None
                    for hh in range(2):
                        h = hp * 2 + hh
                        wsl = wt[:, hh * 2048:(hh + 1) * 2048].rearrange("p (a b) -> p a b", a=16)

                        def out_fn(psap, gc, rb, waits, h=h):
                            return P.op("dve", lambda e: e.scalar_tensor_tensor(
                                out=CK[:, h * 256:(h + 1) * 256], in0=psap, scalar=gc, in1=rb[:], op0=ALU.mult, op1=ALU.mult),
                                waits=waits, sig=True)
                        evm = HN.main(wsl, lambda kc: HT[:, kc, 0:256], 5, out_fn,
                                      first_waits=[ev_w] + ht_ready[0])
                    WR.free[k] = [evm]
                HN.flush()
                ev_o = tokmajor_mm([(0, 8), (8, 8)], w_loader(wv, 512), lambda kc, g: HT[:, kc, g * 128:(g + 1) * 128], NKC, 2)
                for g in range(2):
                    evc = P.op("dve", lambda e, g=g: e.tensor_copy(out=CV[:, g * 512:(g + 1) * 512], in_=ps[g][:, :]),
                               waits=[ev_o[g]], sig=True)
                    bank_free[g] = [evc]
                P.fence()
                P.scope_end()

        def projection(tt):
            col_t = tt * T
            winv = w_in.rearrange("(kc p) n -> p kc n", p=128)
            with ExitStack() as ph:
                P.scope_begin()
                HN = HeadNorm(ph, 512)
                STG = Ring(P, ph, "stg", 3, [128, 512], BF16, dma=True)
                QF = Ring(P, ph, "qf", 2, [128, 512], F32)
                VST = Ring(P, ph, "vst", 4, [128, 512], BF16, dma=True)
                MRS = Ring(P, ph, "mrs", 2, [8, 512], BF16, dma=True)
                gsm = sb("gsm", [128, 3 * 32], F32, ph)
                wff = sb("wff", [128, NKC, 8], BF16, ph)
                eb = sb("eb", [8, 512], F32, ph)
                s_ff = P.dsem()
                chunks = []
                for h in range(4):
                    chunks.append((C_BK + h * 128, 3, KTs, 8 + h, h, None))
                for h in range(8):
                    chunks.append((C_FK + h * 128, 1, KTs, h, None, None))
                for h in range(8):
                    chunks.append((C_FQ + h * 128, 0, QTs, h, None, None))
                for h in range(4):
                    chunks.append((C_BQ + h * 128, 2, QTs, 8 + h, None, h))
                for h in range(4):
                    chunks.append((C_CQ + h * 128, 4, QTs, 12 + h, None, None))
                km_last = [None]
                gsm2 = [gsm, sb("gsm_b", [128, 3 * 32], F32, ph)]
                gsm_free = [[], []]
                gcount = [0]
                deferred = []

                def defer(fn, delay):
                    deferred.append([delay, fn])

                def tick(all_=False):
                    while True:
                        snap = list(deferred)
                        deferred.clear()
                        ran = False
                        for d_ in snap:
                            if all_ or d_[0] <= 0:
                                d_[1]()
                                ran = True
                            else:
                                d_[0] -= 1
                                deferred.append(d_)
                        if not (all_ and deferred):
                            break

                def gating(mq_i, n, kq, qf, ev_qf):
                    gi = gcount[0] % 2
                    gcount[0] += 1
                    gs = gsm2[gi]
                    bg, bt = 5, 6

                    def partA():
                        ev_g = None
                        for g in range(4):
                            ev_g = P.op("pe", lambda e, g=g: e.matmul(
                                ps[bg][:, g * 8:(g + 1) * 8], qf[:, g * 128:(g + 1) * 128], kmT[:, mq_i * 8:(mq_i + 1) * 8],
                                start=True, stop=True), waits=([ev_qf, km_last[0]] + bank_free[bg]) if g == 0 else (), sig=(g == 3))
                        QF.free[kq] = [ev_g]
                        ev_m = P.op("dve", lambda e: e.tensor_tensor(
                            out=gs[:, 0:32], in0=ps[bg][:, 0:32], in1=pastm[:, n * 32:(n + 1) * 32], op=ALU.add),
                            waits=[ev_g] + gsm_free[gi], sig=True)
                        bank_free[bg] = [ev_m]
                        ev_n = None
                        for g in range(4):
                            own = 4 + 2 * n + g // 2
                            e1 = P.op("dve", lambda e, g=g: e.max(out=gs[:, 32 + g * 8:40 + g * 8], in_=gs[:, g * 8:(g + 1) * 8]),
                                      waits=[ev_m], sig=True)
                            e2 = P.op("dve", lambda e, g=g: e.tensor_scalar(
                                out=gs[:, 64 + g * 8:72 + g * 8], in0=gs[:, g * 8:(g + 1) * 8],
                                scalar1=gs[:, 32 + g * 8 + 2:32 + g * 8 + 3], scalar2=NEGB, op0=ALU.is_lt, op1=ALU.mult),
                                waits=[e1], sig=True)
                            ev_n = P.op("dve", lambda e, g=g, own=own: e.memset(gs[:, 64 + g * 8 + own:64 + g * 8 + own + 1], 0.0),
                                        waits=[e2], sig=True)

                        def partB():
                            ev_t = None
                            for g in range(4):
                                ev_t = P.op("pe", lambda e, g=g: e.transpose(
                                    ps[bt][0:8, g * 128:(g + 1) * 128], gs[:, 64 + g * 8:72 + g * 8], identf),
                                    waits=([ev_n] + bank_free[bt]) if g == 0 else (), sig=(g == 3))
                            gsm_free[gi] = [ev_t]
                            km, mrs = MRS.next()
                            ev_c = P.op("act", lambda e: e.copy(out=mrs[:], in_=ps[bt][0:8, :]), waits=[ev_t] + MRS.free[km], sig=True)
                            bank_free[bt] = [ev_c]
                            MRS.free[km] = [P.dma("sp", MRs[mq_i, :, n * 512:(n + 1) * 512], mrs[:], MRS.sem[km], waits=[ev_c])]
                        defer(partB, 1)
                    defer(partA, 1)

                for ci in range(0, len(chunks), 2):
                    k, wt = WR.next()
                    ev_w = None
                    for hh in range(2):
                        co = chunks[ci + hh][0]
                        wsl = wt[:, hh * 2048:(hh + 1) * 2048].rearrange("p (a b) -> p a b", a=16)
                        ev_w = P.dma("pool", wsl, winv[:, :, co:co + 128], WR.sem[k], waits=WR.free[k] if hh == 0 else ())
                    evm = None
                    for hh in range(2):
                        co, gidx, dscr_, dh, mk_i, mq_i = chunks[ci + hh]
                        wsl = wt[:, hh * 2048:(hh + 1) * 2048].rearrange("p (a b) -> p a b", a=16)
                        for n in range(2):
                            def out_fn(psap, gc, rb, waits, dscr_=dscr_, dh=dh, n=n, mk_i=mk_i, mq_i=mq_i):
                                ksg, stg = STG.next()
                                ev_q = P.op("dve", lambda e: e.scalar_tensor_tensor(
                                    out=stg[:], in0=psap, scalar=gc, in1=rb[:], op0=ALU.mult, op1=ALU.mult),
                                    waits=waits + STG.free[ksg], sig=True)
                                ev_last = ev_q
                                if mk_i is not None:
                                    blk = (col_t + n * 512) // 256
                                    ev_last = P.op("dve", lambda e: e.tensor_reduce(
                                        out=kmT[:, mk_i * 8 + blk:mk_i * 8 + blk + 2],
                                        in_=stg[:].rearrange("p (a b) -> p a b", a=2), axis=AX.X, op=ALU.add),
                                        waits=[ev_q], sig=True)
                                    km_last[0] = ev_last
                                ev_st = P.dma("sp", dscr_[dh, :, col_t + n * 512:col_t + (n + 1) * 512], stg[:], STG.sem[ksg], waits=[ev_q])
                                STG.free[ksg] = [ev_st]
                                if mq_i is not None and tt == 1:
                                    kq, qf = QF.next()
                                    ev_last = P.op("dve", lambda e: e.scalar_tensor_tensor(
                                        out=qf[:], in0=psap, scalar=gc, in1=rb[:], op0=ALU.mult, op1=ALU.mult),
                                        waits=QF.free[kq], sig=True)
                                    gating(mq_i, n, kq, qf, ev_last)
                                return ev_last
                            evm = HN.main(wsl, lambda kc, n=n: HT[:, kc, n * 512:(n + 1) * 512], gidx, out_fn,
                                          first_waits=([ev_w] + ht_ready[0]) if (hh == 0 and n == 0) else ())
                            tick()
                    WR.free[k] = [evm]
                HN.flush()
                tick(all_=True)
                for vs, co in enumerate((C_FV, C_FV + 512, C_BV)):
                    ev_o = tokmajor_mm([(0, 8), (8, 8)], w_loader(winv, co), lambda kc, g: HT[:, kc, g * 128:(g + 1) * 128], NKC, 8)
                    for g in range(8):
                        kv, vst = VST.next()
                        if g % 2:
                            ev_c = P.op("act", lambda e, vst=vst, g=g: e.copy(out=vst[:], in_=ps[g][:, :]), waits=[ev_o[g]] + VST.free[kv], sig=True)
                        else:
                            ev_c = P.op("dve", lambda e, vst=vst, g=g: e.tensor_copy(out=vst[:], in_=ps[g][:, :]), waits=[ev_o[g]] + VST.free[kv], sig=True)
                        bank_free[g] = [ev_c]
                        VST.free[kv] = [P.dma("sp", Vs[col_t + g * 128:col_t + (g + 1) * 128, vs * 512:(vs + 1) * 512], vst[:], VST.sem[kv], waits=[ev_c])]
                ev_w = P.dma("pool", wff[:], winv[:, :, C_FF:C_FF + 8], s_ff, waits=list(P.last_fence))
                ev_sp = None
                for n in range(2):
                    b = n
                    ev = None
                    for kc in range(NKC):
                        wl = ([ev_w] + bank_free[b]) if kc == 0 else []
                        ev = P.op("pe", lambda e, b=b, kc=kc, n=n: e.matmul(
                            ps[b][0:8, :], wff[:, kc, :], HT[:, kc, n * 512:(n + 1) * 512],
                            start=(kc == 0), stop=(kc == NKC - 1)), waits=wl, sig=(kc == NKC - 1))
                    ev_e = P.op("act", lambda e, b=b: e.activation(out=eb[:], in_=ps[b][0:8, :], func=AF.Exp, bias=negb[:, 0:1], scale=-1.0),
                                waits=[ev, ev_sp], sig=True)
                    bank_free[b] = [ev_e]
                    ev_sp = P.op("act", lambda e, n=n: e.activation(
                        out=SPT[:, col_t + n * 512:col_t + (n + 1) * 512], in_=eb[:], func=AF.Ln, bias=oneT[0:8, 0:1], scale=1.0),
                        waits=[ev_e], sig=True)
                P.fence()
                P.scope_end()

        def cprep_body(ph):
            if True:
                onesr = sb("onesr", [8, S], F32, ph)
                Cp = sb("Cp", [8, S], F32, ph)
                ctp = sb("ctp", [8, S], F32, ph)
                r1 = sb("r1", [8, S], F32, ph)
                hi = sb("hi", [8, S], BF16, ph)
                mid = sb("mid", [8, S], BF16, ph)
                lo = sb("lo", [8, S], BF16, ph)
                sc = P.dsem()
                e0 = P.op("dve", lambda e: e.memset(onesr[:], 1.0), sig=True)
                e1 = P.op("dve", lambda e: e.tensor_tensor_scan(out=Cp[:], data0=onesr[:], data1=SPT[:], initial=0.0, op0=ALU.mult, op1=ALU.add),
                          waits=[e0], sig=True)
                e2 = P.op("dve", lambda e: e.tensor_scalar(out=ctp[:], in0=Cp[:], scalar1=-RS, scalar2=None, op0=ALU.mult), waits=[e1], sig=True)
                e3 = P.op("dve", lambda e: e.tensor_copy(out=hi[:], in_=ctp[:]), waits=[e2], sig=True)
                e4 = P.op("dve", lambda e: e.tensor_tensor(out=r1[:], in0=ctp[:], in1=hi[:], op=ALU.subtract), waits=[e3], sig=True)
                e5 = P.op("dve", lambda e: e.tensor_copy(out=mid[:], in_=r1[:]), waits=[e4], sig=True)
                e6 = P.op("dve", lambda e: e.tensor_tensor(out=ctp[:], in0=r1[:], in1=mid[:], op=ALU.subtract), waits=[e5], sig=True)
                e7 = P.op("dve", lambda e: e.tensor_copy(out=lo[:], in_=ctp[:]), waits=[e6], sig=True)
                P.dma("sp", cs3[0], hi[:], sc, waits=[e3])
                P.dma("sp", cs3[1], mid[:], sc, waits=[e5])
                P.dma("sp", cs3[2], lo[:], sc, waits=[e7])
                res = {"st": (sc, sc.count), "cpc": None}

                def part2():
                    ev = None
                    for g in range(16):
                        ev = P.op("pe", lambda e, g=g: e.transpose(ps[0][:, g * 8:(g + 1) * 8], Cp[:, g * 128:(g + 1) * 128], identf[0:8, 0:8]),
                                  waits=([e1] + bank_free[0]) if g == 0 else (), sig=(g == 15))
                    evc = P.op("dve", lambda e: e.tensor_copy(out=CpC[:], in_=ps[0][:, 0:128]), waits=[ev], sig=True)
                    bank_free[0] = [evc]
                    res["cpc"] = evc
                res["part2"] = part2
                return res

        def attention():
            with ExitStack() as ph:
                P.scope_begin()
                QR = Ring(P, ph, "aq", 2, [128, S], BF16, dma=True)
                KR = Ring(P, ph, "ak", 2, [128, S], BF16, dma=True)
                VR = Ring(P, ph, "av", 2, [128, 16, 128], BF16, dma=True)
                CR = Ring(P, ph, "ac", 2, [128, S], BF16, dma=True)
                MR = Ring(P, ph, "am", 2, [128, 1024], BF16, dma=True)
                for ring in (CR, MR):
                    for i_, t_ in enumerate(ring.t):
                        ring.free[i_] = [P.op("dve", lambda e, t_=t_: e.memset(t_[:], 0.0), sig=True)]
                PT = Ring(P, ph, "pt", 6, [128, 512], BF16)
                RL = Ring(P, ph, "rl", 2, [128, 512], F32)
                OS = Ring(P, ph, "os", 2, [128, 512], BF16, dma=True)
                ACC = Ring(P, ph, "acc", 2, [128, 512], F32)
                ACD = Ring(P, ph, "acd", 2, [128, 512], F32)
                ACB = Ring(P, ph, "acb", 2, [128, 512], BF16)
                s_ot = P.dsem()
                zt = sb("zt", [128, 512], BF16, ph)
                ev_zt = P.op("dve", lambda e: e.memset(zt[:], 0.0), sig=True)
                Vsv = Vs.rearrange("(g p) c -> p g c", p=128)
                Gm = sb("Gm", [128, 4 * GW], BF16, ph)
                s_gm = P.dsem()
                ev_gm = None
                for h_ in range(4):
                    ev_gm = P.dma("sp", Gm[:, h_ * GW:(h_ + 1) * GW], bass.AP(gv2d.tensor, h_ * GV, [[1, 128], [1, GW]]), s_gm)

                def kind_of(h):
                    return "fox" if h < 8 else ("moba" if h < 12 else "mem")

                def loads(h):
                    kind = kind_of(h)
                    d = {}
                    kq, qt = QR.next()
                    d["q"] = (kq, qt, P.dma("sp", qt[:], QTs[h], QR.sem[kq], waits=QR.free[kq]))
                    if kind != "mem":
                        kk, kt = KR.next()
                        d["k"] = (kk, kt, P.dma("sp", kt[:], KTs[h], KR.sem[kk], waits=KR.free[kk]))
                        kv, vt = VR.next()
                        d["v"] = (kv, vt, P.dma("sp", vt[:], Vsv[:, :, h * 128:(h + 1) * 128], VR.sem[kv], waits=VR.free[kv]))
                    if kind == "fox":
                        kc_, ct = CR.next()
                        d["c"] = (kc_, ct, P.dma("sp", ct[0:3, :], cs3[:, h, :], CR.sem[kc_], waits=CR.free[kc_] + [cp_evs["st"]]))
                    if kind == "moba":
                        km, mt = MR.next()
                        d["m"] = (km, mt, P.dma("sp", mt[0:8, :], MRs[h - 8], MR.sem[km], waits=MR.free[km]))
                    return d

                tcount = [0]

                def head(h, d):
                    kind = kind_of(h)
                    qt = d["q"][1]
                    ld_evs = [v[2] for v in d.values()] + [ev_gm] + ([cp_evs["cpc"]] if kind == "fox" else [])
                    tiles = []
                    for I in range(4):
                        nk = 2 if kind == "mem" else 4 * I + 4
                        for j in range(nk):
                            tiles.append((I, j, nk))
                    ev_s = {}
                    ev_last_pe = [None]
                    qs = {}
                    st_evs = {}
                    pendingL = []

                    def tickL(all_=False):
                        for it in list(pendingL):
                            if all_ or it[0] <= 0:
                                pendingL.remove(it)
                                it[1]()
                            else:
                                it[0] -= 1

                    def c0_of(I, j):
                        if kind == "mem":
                            return 0
                        return max(0, 128 * (j - 4 * I))

                    def emit_S(idx):
                        I, j, nk = tiles[idx]
                        b = (0, 1, 2, 7)[tcount[0] % 4]
                        tcount[0] += 1
                        c0 = c0_of(I, j)
                        if kind == "mem":
                            lhs = CK[:, (h - 12) * 256 + j * 128:(h - 12) * 256 + (j + 1) * 128]
                        else:
                            lhs = d["k"][1][:, j * 128:(j + 1) * 128]
                        rhs = qt[:, I * 512 + c0:(I + 1) * 512]
                        outp = ps[b][:, c0:512]
                        extra = []
                        if kind == "fox":
                            ct = d["c"][1]
                            extra.append((ones3z, ct[:, I * 512 + c0:(I + 1) * 512]))
                            if j >= 4 * I:
                                off = 384 - 128 * (j - 4 * I)
                                extra.append((antib, mstrip[:, off + c0:off + 512]))
                        elif kind == "moba":
                            off = min(512 * I - 128 * j + 384, 640)
                            extra.append((antib, Gm[:, (h - 8) * GW + off + c0:(h - 8) * GW + off + 512]))
                            if I >= 2:
                                J = j // 2
                                mt = d["m"][1]
                                extra.append((esel[:, J * 128:(J + 1) * 128], mt[:, (I - 2) * 512 + c0:(I - 1) * 512]))
                        wl = bank_free[b] + (ld_evs if idx == 0 else [])
                        ev = P.op("pe", lambda e: e.matmul(outp, lhs, rhs, start=True, stop=(len(extra) == 0)),
                                  waits=wl, sig=(len(extra) == 0))
                        for xi, (l2, r2) in enumerate(extra):
                            last = xi == len(extra) - 1
                            ev = P.op("pe", lambda e, l2=l2, r2=r2, last=last: e.matmul(outp, l2, r2, start=False, stop=last),
                                      sig=last)
                        ev_s[idx] = (b, ev)

                    def emit_PV(idx):
                        I, j, nk = tiles[idx]
                        b, ev = ev_s[idx]
                        c0 = c0_of(I, j)
                        kp, pt = PT.next()
                        if kind == "fox":
                            bias = CpC[:, j * 8 + h:j * 8 + h + 1]
                            ev_x = P.op("act", lambda e: e.activation(out=pt[:, c0:512], in_=ps[b][:, c0:512], func=AF.Exp, bias=bias, scale=SCALE),
                                        waits=[ev] + PT.free[kp], sig=True)
                        else:
                            ev_x = P.op("act", lambda e: e.activation(out=pt[:, c0:512], in_=ps[b][:, c0:512], func=AF.Exp, scale=SCALE),
                                        waits=[ev] + PT.free[kp], sig=True)
                        bank_free[b] = [ev_x]
                        bo, bl = 3 + I % 2, 5 + I % 2
                        if kind == "mem":
                            vl = CV[:, j * 512 + (h - 12) * 128:j * 512 + (h - 12 + 1) * 128]
                        else:
                            vl = d["v"][1][:, j, :]
                        ev_p = P.op("pe", lambda e: e.matmul(ps[bo][:, c0:512], vl, pt[:, c0:512], start=(j == 0), stop=(j == nk - 1)),
                                    waits=[ev_x] + (bank_free[bo] if j == 0 else []), sig=True)
                        ev_last_pe[0] = ev_p
                        if j == 0:
                            qs["kb"], qs["acb"] = ACB.next()
                            qs["ka"], qs["accP"] = ACC.next()
                            _, qs["accD"] = ACD.next()
                            qs["evP"] = None
                            qs["evD"] = None
                            qs["cD"] = None
                            qs["ev"] = None
                        acb, accP, accD = qs["acb"], qs["accP"], qs["accD"]
                        npe = 3 if nk >= 8 else nk
                        if j < nk - npe:
                            if j % 3 == 0:
                                if j == 0:
                                    qs["evP"] = P.op("pool", lambda e: e.tensor_tensor(out=accP[:], in0=pt[:], in1=zt[:], op=ALU.add),
                                                     waits=[ev_x, ev_zt] + ACC.free[qs["ka"]], sig=True)
                                else:
                                    qs["evP"] = P.op("pool", lambda e: e.tensor_tensor(out=accP[:, c0:512], in0=accP[:, c0:512], in1=pt[:, c0:512], op=ALU.add),
                                                     waits=[ev_x, qs["evP"]], sig=True)
                                ev_acc = qs["evP"]
                            else:
                                if qs["cD"] is None:
                                    qs["cD"] = c0
                                    qs["evD"] = P.op("dve", lambda e: e.tensor_tensor(out=accD[:, c0:512], in0=pt[:, c0:512], in1=zt[:, c0:512], op=ALU.add),
                                                     waits=[ev_x], sig=True)
                                else:
                                    qs["evD"] = P.op("dve", lambda e: e.tensor_tensor(out=accD[:, c0:512], in0=accD[:, c0:512], in1=pt[:, c0:512], op=ALU.add),
                                                     waits=[ev_x, qs["evD"]], sig=True)
                                ev_acc = qs["evD"]
                            PT.free[kp] = [ev_p, ev_acc]
                            if j == nk - npe - 1:
                                cD = qs["cD"]
                                wl = [qs["evP"], qs["evD"]] + ACB.free[qs["kb"]]
                                if cD is None:
                                    ev_cb = P.op("dve", lambda e: e.tensor_copy(out=acb[:], in_=accP[:]), waits=wl, sig=True)
                                else:
                                    if cD > 0:
                                        P.op("dve", lambda e: e.tensor_copy(out=acb[:, 0:cD], in_=accP[:, 0:cD]), waits=wl)
                                        wl = []
                                    ev_cb = P.op("dve", lambda e: e.tensor_tensor(out=acb[:, cD:512], in0=accP[:, cD:512], in1=accD[:, cD:512], op=ALU.add),
                                                 waits=wl, sig=True)
                                qs["ev"] = ev_cb
                                ACC.free[qs["ka"]] = [ev_cb]
                        else:
                            first = (j == nk - npe)
                            has_acc = npe < nk
                            ev_l = P.op("pe", lambda e: e.matmul(ps[bl][:, c0:512], onesb, pt[:, c0:512], start=first,
                                                                 stop=(j == nk - 1 and not has_acc)),
                                        waits=bank_free[bl] if first else (), sig=True)
                            PT.free[kp] = [ev_l]
                            if j == nk - 1 and has_acc:
                                ev_l = P.op("pe", lambda e: e.matmul(ps[bl][:, :], onesb, acb[:], start=False, stop=True),
                                            waits=[qs["ev"]], sig=True)
                                ACB.free[qs["kb"]] = [ev_l]
                            ev_last_pe[0] = ev_l
                        if j == nk - 1:

                            def finish():
                                kr, rl = RL.next()
                                ev_r0 = P.op("act", lambda e: e.activation(out=rl[:], in_=ps[bl][:, :], func=AF.Ln), waits=[ev_l] + RL.free[kr], sig=True)
                                bank_free[bl] = [ev_r0]
                                ev_r = P.op("act", lambda e: e.activation(out=rl[:], in_=rl[:], func=AF.Exp, scale=-1.0), waits=[ev_r0], sig=True)
                                ko, ot = OS.next()
                                ev_o = P.op("dve", lambda e: e.tensor_tensor(out=ot[:], in0=ps[bo][:, :], in1=rl[:], op=ALU.mult),
                                            waits=[ev_r, ev_p] + OS.free[ko], sig=True)
                                bank_free[bo] = [ev_o]
                                RL.free[kr] = [ev_o]
                                ev_st = P.dma("sp", OTs[h, :, I * 512:(I + 1) * 512], ot[:], OS.sem[ko], waits=[ev_o])
                                OS.free[ko] = [ev_st]
                                st_evs[I] = ev_st

                            pendingL.append([1 if kind == "mem" else 2, finish])

                    for i0 in range(min(3, len(tiles))):
                        emit_S(i0)
                    for idx in range(len(tiles)):
                        if idx + 3 < len(tiles):
                            emit_S(idx + 3)
                        emit_PV(idx)
                        tickL()
                    tickL(all_=True)
                    for key, ring in (("q", QR), ("k", KR), ("v", VR), ("c", CR), ("m", MR)):
                        if key in d:
                            ring.free[d[key][0]] = [ev_last_pe[0]]
                    P.dma("sp", HT[:, h, :], OTs[h, :, 0:T], s_ot, waits=[st_evs[0], st_evs[1]])

                horder = [12, 13, 14, 15, 8, 9, 10, 11, 0, 1, 2, 3, 4, 5, 6, 7]
                nxt = loads(horder[0])
                cp_evs = cprep_body(ph)
                for hi_, h in enumerate(horder):
                    if h == 0:
                        cp_evs["part2"]()
                    cur = nxt
                    if hi_ + 1 < 16:
                        nxt = loads(horder[hi_ + 1])
                    head(h, cur)
                P.fence()
                P.scope_end()

        def outproj_both():
            wov = w_out.rearrange("(kc p) n -> p kc n", p=128)
            with ExitStack() as ph:
                P.scope_begin()
                XP = Ring(P, ph, "xq", 16, [128, 512], F32, dma=True)
                HT2 = sb("HT2", [128, NKC, T], BF16, ph)
                so = P.dsem()
                ev_ot = P.dma("sp", HT2[:, :, :], OTs.rearrange("h p t -> p h t")[:, :, T:2 * T], so)
                for tt in range(NT):
                    rows = slice(tt * T, (tt + 1) * T)
                    src = HT if tt == 0 else HT2
                    res_rows = x1s[rows, :]
                    dst_rows = x2s[rows, :]
                    for s in range(4):
                        xl = []
                        for g in range(8):
                            kx, xt_ = XP.next()
                            xl.append((kx, xt_, P.dma("sp", xt_[:], res_rows[g * 128:(g + 1) * 128, s * 512:(s + 1) * 512], XP.sem[kx], waits=XP.free[kx])))
                        ev_o = tokmajor_mm([(0, 8), (8, 8)], w_loader(wov, s * 512), lambda kc, g, src=src: src[:, kc, g * 128:(g + 1) * 128], NKC, 8,
                                           first_waits=[ev_ot] if (s == 0 and tt == 1) else ())
                        for g in range(8):
                            kx, xt_, ev_x = xl[g]
                            ev_e = P.op("dve", lambda e, xt_=xt_, g=g: e.tensor_tensor(out=xt_[:], in0=ps[g][:, :], in1=xt_[:], op=ALU.add),
                                        waits=[ev_o[g], ev_x], sig=True)
                            bank_free[g] = [ev_e]
                            XP.free[kx] = [P.dma("sp", dst_rows[g * 128:(g + 1) * 128, s * 512:(s + 1) * 512], xt_[:], XP.sem[kx], waits=[ev_e])]
                P.fence()
                P.scope_end()

        r0, r1 = slice(0, T), slice(T, 2 * T)
        phase_mem()
        normpass(x[r0, :], 0, T)
        ffn(w_ffn[0], x[r0, :], x1s[r0, :], bg=(x[r1, :], 0), ht_next=True)
        ffn(w_ffn[0], x[r1, :], x1s[r1, :], bg=(x1s[r0, :], 1), ht_next=True)
        projection(0)
        normpass(x1s[r1, :], 1, T)
        projection(1)
        attention()
        outproj_both()
        normpass(x2s[r0, :], 2, T)
        ffn(w_ffn[1], x2s[r0, :], out[r0, :], bg=(x2s[r1, :], 2), ht_next=True)
        ffn(w_ffn[1], x2s[r1, :], out[r1, :])
        P.fence()
        block = es.enter_context(nc.Block())
        P.replay(block)
    return nc


_CONSTS = None


def make_in_maps(inputs, n_cores=8):
    global _CONSTS
    if _CONSTS is None:
        _CONSTS = host_consts()
    cb16, cf32, esel, oh, negv = _CONSTS
    f = lambda a: np.ascontiguousarray(np.asarray(a, dtype=np.float32))
    norms = np.stack([f(inputs["ffn1_norm"])[0], f(inputs["mix_norm"])[0], f(inputs["ffn2_norm"])[0], f(inputs["mem_norm"])[0]], axis=0)
    normsT = norms.reshape(4, 16, 128).transpose(2, 0, 1).reshape(128, 64)
    gains = np.stack([f(inputs[k])[0] for k in ("fox_q_gain", "fox_k_gain", "moba_q_gain", "moba_k_gain", "mem_q_gain", "mem_k_gain")], axis=1)
    shared = {
        "ffn1_w1": f(inputs["ffn1_w1"])[0], "ffn1_w3": f(inputs["ffn1_w3"])[0], "ffn1_w2": f(inputs["ffn1_w2"])[0],
        "ffn2_w1": f(inputs["ffn2_w1"])[0], "ffn2_w3": f(inputs["ffn2_w3"])[0], "ffn2_w2": f(inputs["ffn2_w2"])[0],
        "w_in": f(inputs["w_in"])[0], "w_mem_kv": f(inputs["w_mem_kv"])[0], "w_out": f(inputs["w_out"])[0],
        "normsT": np.ascontiguousarray(normsT), "gains": np.ascontiguousarray(gains),
        "b_forget": f(inputs["b_forget"]).reshape(8, 1), "rel_bias": f(inputs["rel_bias"]),
        "cb16": cb16, "cf32": cf32, "esel": esel, "oh": oh, "negv": negv,
    }
    xs = f(inputs["x"])
    ms = f(inputs["mem"])
    maps = []
    for c in range(n_cores):
        m = dict(shared)
        m["x"] = xs[c]
        m["mem"] = ms[c]
        maps.append(m)
    return maps


_NC = None


def kernel(**inputs):
    global _NC
    if _NC is None:
        _NC = build()
    maps = make_in_maps(inputs, 8)
    res = run_bass_kernel_spmd(_NC, maps, core_ids=list(range(8)))
    return np.stack([np.asarray(r["out"], dtype=np.float32) for r in res.results], axis=0)
```

```python
import math
from contextlib import ExitStack

import numpy as np
import ml_dtypes

import concourse.bass as bass
import concourse.mybir as mybir
from concourse.bass_utils import run_bass_kernel_spmd

F32 = mybir.dt.float32
BF16 = mybir.dt.bfloat16
AF = mybir.ActivationFunctionType
ALU = mybir.AluOpType
AX = mybir.AxisListType

D = 2048
S = 2048
DFF = 5632
NKC = D // 128
NFC = DFF // 128
T = 1024
NT = S // T
INW = 5128
EPS = 1e-6
SCALE = 128 ** -0.5
RS = math.sqrt(128.0)
NEGB = -30000.0
GW = 1152
GV = 1280
MW = 896
C_FQ, C_FK, C_FV, C_FF, C_BQ, C_BK, C_BV, C_CQ = 0, 1024, 2048, 3072, 3080, 3592, 4104, 4616


class Sem:
    def __init__(self, h):
        self.h = h
        self.count = 0


class Prog:
    NAMES = ["pe", "act", "dve", "pool", "sp"]

    def __init__(self, nc, es):
        self.nc, self.es = nc, es
        self.ops = {n: [] for n in self.NAMES}
        self.csem = {n: Sem(es.enter_context(nc.semaphore("c_" + n))) for n in self.NAMES}
        self.dsems = []
        self.pending = {n: [] for n in self.NAMES}
        self.nsem = 0
        self.free_sems = []
        self.scopes = []

    def dsem(self):
        if self.free_sems:
            s = self.free_sems.pop()
        else:
            s = Sem(self.es.enter_context(self.nc.semaphore("d%d" % self.nsem)))
            self.nsem += 1
            self.dsems.append(s)
        if self.scopes:
            self.scopes[-1].append(s)
        return s

    def scope_begin(self):
        self.scopes.append([])

    def scope_end(self):
        self.free_sems.extend(self.scopes.pop())

    def op(self, eng, fn, waits=(), sig=False):
        w = [x for x in waits if x is not None] + self.pending[eng]
        self.pending[eng] = []
        ev = None
        s = self.csem[eng]
        if sig:
            s.count += 1
            ev = (s, s.count)
        self.ops[eng].append((fn, w, 1 if sig else 0, s))
        return ev

    def dma(self, eng, out, in_, sem, waits=()):
        w = [x for x in waits if x is not None] + self.pending[eng]
        self.pending[eng] = []
        sem.count += 16
        self.ops[eng].append((lambda e, o=out, i=in_: e.dma_start(out=o, in_=i), w, 16, sem))
        return (sem, sem.count)

    def fence(self):
        evs = []
        evs.append(self.op("act", lambda e: e.activation(out=self.mka[:, 0:1], in_=self.mka[:, 1:2], func=AF.Copy), waits=[self.ev_mka], sig=True))
        evs.append(self.op("dve", lambda e: e.memset(self.mkd[:, 0:1], 0.0), sig=True))
        evs.append(self.op("pe", lambda e: e.matmul(self.mkp[0:8, 0:8], self.mkb[:, 0:8], self.mkb[:, 0:8], start=True, stop=True),
                           waits=list(self.bank_free[7]) + [self.ev_mkb], sig=True))
        for s in self.dsems:
            if s.count:
                evs.append((s, s.count))
        for n in self.NAMES:
            if n != "pool":
                self.pending[n] = list(evs)
        self.last_fence = list(evs)
        return evs

    def replay(self, block):
        def run(e, name):
            waited = {}
            for fn, waits, inc, sem in self.ops[name]:
                for (s, v) in waits:
                    if waited.get(id(s), 0) < v:
                        e.wait_ge(s.h, v)
                        waited[id(s)] = v
                ins = fn(e)
                if inc:
                    ins.then_inc(sem.h, inc)
            for (s, v) in self.pending[name]:
                if waited.get(id(s), 0) < v:
                    e.wait_ge(s.h, v)
                    waited[id(s)] = v

        @block.tensor
        def _(e):
            run(e, "pe")

        @block.scalar
        def _(e):
            run(e, "act")

        @block.vector
        def _(e):
            run(e, "dve")

        @block.gpsimd
        def _(e):
            run(e, "pool")

        @block.sync
        def _(e):
            run(e, "sp")


class Ring:
    def __init__(self, P, es, name, n, shape, dtype, dma=False):
        self.n = n
        self.t = [P.sb("%s%d" % (name, i), shape, dtype, es) for i in range(n)]
        self.free = [[] for _ in range(n)]
        self.sem = [P.dsem() for _ in range(n)] if dma else None
        self.i = 0

    def next(self):
        k = self.i % self.n
        self.i += 1
        return k, self.t[k]


def t5_bucket_np(d):
    d = np.asarray(d)
    n = np.maximum(d, 0)
    nf = np.maximum(n, 1).astype(np.float32)
    large = 16 + (np.log(nf / np.float32(16)) / np.float32(math.log(128 / 16)) * np.float32(16)).astype(np.int32)
    large = np.minimum(large, 31)
    return np.where(n < 16, n, large)


def host_consts():
    bf = ml_dtypes.bfloat16
    ident = np.eye(128, dtype=np.float32)
    anti = ident[::-1].copy()
    onesdiv = np.full((128, 128), 1.0 / 128, np.float32)
    ones = np.ones((128, 128), np.float32)
    k = np.arange(128)[:, None]
    u = np.arange(MW)[None, :]
    mstrip = np.where(k + u - 511 >= 0, 0.0, NEGB).astype(np.float32)
    ones3z = np.zeros((128, 128), np.float32)
    ones3z[0:3, :] = 1.0
    cb16 = np.concatenate([ident, anti, onesdiv, ones, ones3z, mstrip], axis=1).astype(bf)
    pm = np.zeros((128, 2, 4, 8), np.float32)
    for n in range(2):
        for g in range(4):
            own = 4 + 2 * n + g // 2
            pm[:, n, g, own:] = -1e30
    cf32 = np.concatenate([ident, pm.reshape(128, 64)], axis=1).astype(np.float32)
    esel = np.zeros((128, 8, 128), np.float32)
    for j in range(8):
        esel[j, j, :] = 1.0
    esel = esel.reshape(128, 1024).astype(bf)
    i = np.arange(GV)
    dd = i - 511
    bk = t5_bucket_np(dd)
    oh = np.zeros((32, GV), np.float32)
    valid = dd >= 0
    oh[bk[valid], i[valid]] = 1.0
    negv = np.where(valid, 0.0, NEGB).astype(np.float32)[None, :].repeat(4, axis=0)
    return cb16, cf32, esel, oh, negv


def build(debug=False, stop_after=None):
    nc = bass.Bass("TRN2", target_bir_lowering=False)
    ikind = "ExternalOutput" if debug else "Internal"

    def din(name, shape, dt=F32):
        return nc.dram_tensor(name, list(shape), dt, kind="ExternalInput").ap()

    def dscr(name, shape, dt):
        return nc.dram_tensor(name, list(shape), dt, kind=ikind).ap()

    x = din("x", [S, D])
    mem = din("mem", [256, D])
    w_ffn = [(din("ffn1_w1", [D, DFF]), din("ffn1_w3", [D, DFF]), din("ffn1_w2", [DFF, D])),
             (din("ffn2_w1", [D, DFF]), din("ffn2_w3", [D, DFF]), din("ffn2_w2", [DFF, D]))]
    w_in = din("w_in", [D, INW])
    w_mkv = din("w_mem_kv", [D, 1024])
    w_out = din("w_out", [D, D])
    normsT = din("normsT", [128, 64])
    gains = din("gains", [128, 6])
    bfg = din("b_forget", [8, 1])
    relb = din("rel_bias", [32, 4])
    cb16_d = din("cb16", [128, 640 + MW], BF16)
    cf32_d = din("cf32", [128, 192])
    esel_d = din("esel", [128, 1024], BF16)
    oh_d = din("oh", [32, GV])
    negv_d = din("negv", [4, GV])
    out = nc.dram_tensor("out", [S, D], F32, kind="ExternalOutput").ap()

    x1s = dscr("x1s", [S, D], F32)
    x2s = dscr("x2s", [S, D], F32)
    QTs = dscr("QTs", [16, 128, S], BF16)
    KTs = dscr("KTs", [12, 128, S], BF16)
    Vs = dscr("Vs", [S, 1536], BF16)
    OTs = dscr("OTs", [16, 128, S], BF16)
    MRs = dscr("MRs", [4, 8, 1024], BF16)
    cs3 = dscr("cs3", [3, 8, S], BF16)
    gv2d = dscr("gv2d", [4, GV], BF16)
    HTd = dscr("HTd", [128, NKC, T], BF16)

    with ExitStack() as es:
        P = Prog(nc, es)
        _uid = [0]

        def sb(name, shape, dt, st=es):
            _uid[0] += 1
            return st.enter_context(nc.sbuf_tensor("s%d_%s" % (_uid[0], name), list(shape), dt))
        P.sb = sb
        P.mka = sb("mka", [128, 2], F32)
        P.mkd = sb("mkd", [128, 1], F32)
        P.mkb = sb("mkb", [128, 8], BF16)
        cb = sb("cb", [128, 640 + MW], BF16)
        cf = sb("cf", [128, 192], F32)
        esel = sb("esel", [128, 1024], BF16)
        gT = sb("gT", [128, 64], F32)
        gcol = sb("gcol", [128, 6], F32)
        epsT = sb("epsT", [128, 1], F32)
        oneT = sb("oneT", [128, 1], F32)
        negb = sb("negb", [8, 1], F32)
        bft = sb("bft", [8, 1], F32)
        CK = sb("CK", [128, 4 * 256], BF16)
        CV = sb("CV", [128, 2 * 512], BF16)
        SPT = sb("SPT", [8, S], F32)
        kmT = sb("kmT", [128, 32], F32)
        CpC = sb("CpC", [128, 128], F32)
        stat = sb("stat", [128, 12], F32)
        stat2 = sb("stat2", [128, 12], F32)
        HT = sb("HT", [128, NKC, T], BF16)
        identb = cb[:, 0:128]
        antib = cb[:, 128:256]
        onesdiv = cb[:, 256:384]
        onesb = cb[:, 384:512]
        ones3z = cb[:, 512:640]
        mstrip = cb[:, 640:640 + MW]
        identf = cf[:, 0:128]
        pastm = cf[:, 128:192]
        WR = Ring(P, es, "wr", 4, [128, 4096], BF16, dma=True)
        s_ff = P.dsem()
        ps = [es.enter_context(nc.psum_tensor("ps%d" % i, [128, 512], F32)) for i in range(8)]
        psb = [p.bitcast(BF16) for p in ps]
        P.mkp = ps[7]
        bank_free = [[] for _ in range(8)]
        P.bank_free = bank_free
        P.last_fence = []

        s_setup = P.dsem()
        ld = []
        for (o, i) in [(cb[:], cb16_d), (cf[:], cf32_d), (esel[:], esel_d), (gcol[:], gains), (gT[:], normsT), (bft[:], bfg)]:
            ld.append(P.dma("sp", o, i, s_setup))
        ev_setup = ld[-1]
        P.op("dve", lambda e: e.memset(epsT[:], EPS))
        P.op("dve", lambda e: e.memset(oneT[:], 1.0))
        P.ev_mka = P.op("dve", lambda e: e.memset(P.mka[:], 0.0), sig=True)
        P.ev_mkb = P.op("dve", lambda e: e.memset(P.mkb[:], 0.0), sig=True)
        P.op("dve", lambda e: e.memset(kmT[:], 0.0))
        P.op("dve", lambda e: e.tensor_scalar(out=negb[:], in0=bft[:], scalar1=-1.0, scalar2=None, op0=ALU.mult),
             waits=[ev_setup])
        P.fence()

        with ExitStack() as ph:
            P.scope_begin()
            oh = sb("oh", [32, GV], F32, ph)
            negv = sb("negv", [4, GV], F32, ph)
            rbt = sb("rbt", [32, 4], F32, ph)
            gvs = sb("gvs", [4, GV], BF16, ph)
            s1 = P.dsem()
            P.dma("sp", oh[:], oh_d, s1)
            P.dma("sp", negv[:], negv_d, s1)
            ev = P.dma("sp", rbt[:], relb, s1)
            evs = []
            for c in range(0, GV, 512):
                w = min(512, GV - c)
                b = c // 512
                e1 = P.op("pe", lambda e, b=b, c=c, w=w: e.matmul(ps[b][0:4, 0:w], rbt[:, 0:4], oh[:, c:c + w], start=True, stop=True),
                          waits=[ev], sig=True)
                evs.append(P.op("dve", lambda e, b=b, c=c, w=w: e.scalar_tensor_tensor(
                    out=gvs[:, c:c + w], in0=ps[b][0:4, 0:w], scalar=RS, in1=negv[:, c:c + w], op0=ALU.mult, op1=ALU.add),
                    waits=[e1], sig=True))
            ev = P.dma("sp", gv2d, gvs[:], s1, waits=evs)
            P.fence()
            P.scope_end()

        ht_ready = [[]]

        def normpass(src_rows, norm_idx, ntok, col_base=0, extra_waits=()):
            with ExitStack() as ph:
                P.scope_begin()
                XT = Ring(P, ph, "xt", 3, [128, D], F32, dma=True)
                XN = Ring(P, ph, "xn", 3, [128, D], BF16)
                cp_evs = []
                tq = 0
                H = D // 2
                for g in range(ntok // 128):
                    c = g % 4
                    k, xt = XT.next()
                    ev_ld = P.dma("sp", xt[:], src_rows[g * 128:(g + 1) * 128, :], XT.sem[k],
                                  waits=XT.free[k] + list(extra_waits))
                    kn, xn = XN.next()
                    ev_sq = P.op("act", lambda e, xn=xn, xt=xt, c=c: e.activation(
                        out=xn[:], in_=xt[:], func=AF.Square, accum_out=stat[:, c:c + 1]),
                        waits=[ev_ld] + XN.free[kn], sig=True)
                    ev_ln = P.op("act", lambda e, c=c: e.activation(
                        out=stat[:, 4 + c:5 + c], in_=stat[:, c:c + 1], func=AF.Ln, bias=epsT[:, 0:1], scale=1.0 / D),
                        waits=[ev_sq], sig=True)
                    ev_r = P.op("act", lambda e, c=c: e.activation(
                        out=stat[:, 8 + c:9 + c], in_=stat[:, 4 + c:5 + c], func=AF.Exp, scale=-0.5),
                        waits=[ev_ln], sig=True)
                    ev_a = P.op("act", lambda e, xn=xn, xt=xt, c=c: e.activation(
                        out=xn[:, 0:H], in_=xt[:, 0:H], func=AF.Copy, scale=stat[:, 8 + c:9 + c]),
                        waits=[ev_r], sig=True)
                    ev_d = P.op("dve", lambda e, xn=xn, xt=xt, c=c: e.tensor_scalar(
                        out=xn[:, H:D], in0=xt[:, H:D], scalar1=stat[:, 8 + c:9 + c], scalar2=None, op0=ALU.mult),
                        waits=[ev_r], sig=True)
                    XT.free[k] = [ev_a, ev_d]
                    ev_t = None
                    for q in range(4):
                        b = tq % 4
                        tq += 1
                        for i in range(4):
                            kc = 4 * q + i
                            ev_t = P.op("pe", lambda e, b=b, i=i, kc=kc, xn=xn: e.transpose(
                                psb[b][:, i * 128:(i + 1) * 128], xn[:, kc * 128:(kc + 1) * 128], identb),
                                waits=([ev_a if q < 2 else ev_d] + bank_free[b]) if i == 0 else (), sig=(i == 3))
                        dst = HT[:, 4 * q:4 * q + 4, col_base + g * 128:col_base + (g + 1) * 128]
                        srcv = psb[b][:, 0:512].rearrange("p (a b) -> p a b", a=4)
                        gv = gT[:, norm_idx * 16 + 4 * q:norm_idx * 16 + 4 * q + 4].unsqueeze(2).broadcast_to([128, 4, 128])
                        ev_c = P.op("dve", lambda e, dst=dst, srcv=srcv, gv=gv: e.tensor_tensor(out=dst, in0=srcv, in1=gv, op=ALU.mult),
                                    waits=[ev_t], sig=True)
                        bank_free[b] = [ev_c]
                        cp_evs.append(ev_c)
                    XN.free[kn] = [ev_t]
                ht_ready[0] = cp_evs[-1:]
                P.fence()
                P.scope_end()

        def tokmajor_mm(groups, load_fn, lhs_fn, NK, NG, first_waits=()):
            ev_o = [None] * NG
            for si, (kc0, nk) in enumerate(groups):
                k, wv, ev_w = load_fn(kc0, nk)
                lastgrp = (kc0 + nk == NK)
                if lastgrp:
                    order = [(kk, g) for g in range(NG) for kk in range(nk)]
                else:
                    order = [(kk, g) for kk in range(nk) for g in range(NG)]
                ev = None
                for oi, (kk, g) in enumerate(order):
                    kc = kc0 + kk
                    wl = []
                    if oi == 0:
                        wl = [ev_w] + (list(first_waits) if si == 0 else [])
                    if kc == 0:
                        wl = wl + bank_free[g]
                    islast = oi == len(order) - 1
                    ev = P.op("pe", lambda e, g=g, kc=kc, kk=kk, wv=wv: e.matmul(
                        ps[g][:, :], lhs_fn(kc, g), wv[:, kk, :], start=(kc == 0), stop=(kc == NK - 1)),
                        waits=wl, sig=(kc == NK - 1) or islast)
                    if kc == NK - 1:
                        ev_o[g] = ev
                WR.free[k] = [ev]
            return ev_o

        def w_loader(wview, col0):
            def load_fn(kc0, nk):
                k, wt = WR.next()
                wv = wt[:, 0:nk * 512].rearrange("p (a b) -> p a b", a=nk)
                ev_w = P.dma("pool", wv, wview[:, kc0:kc0 + nk, col0:col0 + 512], WR.sem[k], waits=WR.free[k])
                return k, wv, ev_w
            return load_fn

        def ffn(wset, res_rows, dst_rows, bg=None, ht_next=False):
            w1, w3, w2 = wset
            w1v = w1.rearrange("(kc p) n -> p kc n", p=128)
            w3v = w3.rearrange("(kc p) n -> p kc n", p=128)
            w2v = w2.rearrange("(kc p) n -> p kc n", p=128)
            with ExitStack() as ph:
                P.scope_begin()
                G = sb("G", [128, NFC, T], BF16, ph)
                SIL = Ring(P, ph, "sil", 4, [128, 512], BF16)
                XP = Ring(P, ph, "xp", 14, [128, 512], F32, dma=True)
                pieces = [(s, g) for s in range(4) for g in range(8)]

                def load_piece(i):
                    s, g = pieces[i]
                    k, t = XP.next()
                    return k, t, P.dma("sp", t[:], res_rows[g * 128:(g + 1) * 128, s * 512:(s + 1) * 512], XP.sem[k], waits=XP.free[k])
                xp_q = [load_piece(i) for i in range(8)]

                bg_done = []
                if bg is not None:
                    bsrc, bidx = bg
                    flat = lambda ap: ap.rearrange("p a b -> p (a b)")
                    bxt = [flat(G[:, 28:32, :]).bitcast(F32), flat(G[:, 32:36, :]).bitcast(F32)]
                    bxn = [flat(G[:, 36:38, :]), flat(G[:, 38:40, :])]
                    bst = [flat(G[:, 40:42, :]).rearrange("p (k t) -> p k t", k=16), flat(G[:, 42:44, :]).rearrange("p (k t) -> p k t", k=16)]
                    bsx = [P.dsem(), P.dsem()]
                    bss = [P.dsem(), P.dsem()]
                    bxt_free = [[], []]
                    bxn_free = [[], []]
                    bst_free = [[], []]
                    bstate = {}
                    Hh = D // 2

                    def bgA(g):
                        k = g % 2
                        c = g % 4
                        xt, xn = bxt[k], bxn[k]
                        ev_ld = P.dma("sp", xt, bsrc[g * 128:(g + 1) * 128, :], bsx[k], waits=bxt_free[k])
                        ev_sq = P.op("act", lambda e: e.activation(out=xn, in_=xt, func=AF.Square, accum_out=stat2[:, c:c + 1]),
                                     waits=[ev_ld] + bxn_free[k], sig=True)
                        ev_ln = P.op("act", lambda e: e.activation(out=stat2[:, 4 + c:5 + c], in_=stat2[:, c:c + 1], func=AF.Ln,
                                                                   bias=epsT[:, 0:1], scale=1.0 / D), waits=[ev_sq], sig=True)
                        ev_r = P.op("act", lambda e: e.activation(out=stat2[:, 8 + c:9 + c], in_=stat2[:, 4 + c:5 + c], func=AF.Exp, scale=-0.5),
                                    waits=[ev_ln], sig=True)
                        ev_a = P.op("act", lambda e: e.activation(out=xn[:, 0:Hh], in_=xt[:, 0:Hh], func=AF.Copy, scale=stat2[:, 8 + c:9 + c]),
                                    waits=[ev_r], sig=True)
                        ev_d = P.op("dve", lambda e: e.tensor_scalar(out=xn[:, Hh:D], in0=xt[:, Hh:D], scalar1=stat2[:, 8 + c:9 + c], scalar2=None, op0=ALU.mult),
                                    waits=[ev_r], sig=True)
                        bxt_free[k] = [ev_a, ev_d]
                        bstate[g] = (ev_a, ev_d)

                    def bgB(g):
                        k = g % 2
                        xn, st = bxn[k], bst[k]
                        ev_a, ev_d = bstate.pop(g)
                        ev_t = None
                        ev_c = None
                        for q in range(4):
                            b = 6 + q % 2
                            for i in range(4):
                                kc = 4 * q + i
                                ev_t = P.op("pe", lambda e, b=b, i=i, kc=kc: e.transpose(
                                    psb[b][:, i * 128:(i + 1) * 128], xn[:, kc * 128:(kc + 1) * 128], identb),
                                    waits=([ev_a if q < 2 else ev_d] + bank_free[b]) if i == 0 else (), sig=(i == 3))
                            srcv = psb[b][:, 0:512].rearrange("p (a b) -> p a b", a=4)
                            gv = gT[:, bidx * 16 + 4 * q:bidx * 16 + 4 * q + 4].unsqueeze(2).broadcast_to([128, 4, 128])
                            dst = st[:, 4 * q:4 * q + 4, :]
                            ev_c = P.op("dve", lambda e, dst=dst, srcv=srcv, gv=gv: e.tensor_tensor(out=dst, in0=srcv, in1=gv, op=ALU.mult),
                                        waits=[ev_t] + (bst_free[k] if q == 0 else []), sig=True)
                            bank_free[b] = [ev_c]
                        bxn_free[k] = [ev_t]
                        ev_s = P.dma("sp", HTd[:, :, g * 128:(g + 1) * 128], st, bss[k], waits=[ev_c])
                        bst_free[k] = [ev_s]
                        bg_done.append(ev_s)
                        bg_done.append(ev_t)

                def bg_tick(j):
                    if bg is None:
                        return
                    if j % 3 == 0 and j // 3 < 8:
                        bgA(j // 3)
                    if j % 3 == 2 and j // 3 < 8:
                        bgB(j // 3)

                g_evs = []
                ev_last_s1 = None
                for j in range(NFC):
                    k, wt = WR.next()
                    w1s = wt[:, 0:2048].rearrange("p (a b) -> p a b", a=16)
                    w3s = wt[:, 2048:4096].rearrange("p (a b) -> p a b", a=16)
                    P.dma("pool", w1s, w1v[:, :, j * 128:(j + 1) * 128], WR.sem[k], waits=WR.free[k])
                    ev_w = P.dma("pool", w3s, w3v[:, :, j * 128:(j + 1) * 128], WR.sem[k])
                    first = True
                    for n in range(2):
                        u = 2 * j + n
                        pa, pb = 2 * (u % 3), 2 * (u % 3) + 1
                        ev_mm = {}
                        for wi, ws in enumerate((w1s, w3s)):
                            b = pa if wi == 0 else pb
                            ev = None
                            for kc in range(NKC):
                                wl = []
                                if first:
                                    wl = [ev_w] + (ht_ready[0] if j == 0 else [])
                                    first = False
                                if kc == 0:
                                    wl = wl + bank_free[b]
                                ev = P.op("pe", lambda e, b=b, ws=ws, kc=kc, n=n: e.matmul(
                                    ps[b][:, :], ws[:, kc, :], HT[:, kc, n * 512:(n + 1) * 512],
                                    start=(kc == 0), stop=(kc == NKC - 1)), waits=wl, sig=(kc == NKC - 1))
                            ev_mm[wi] = ev
                        ev_last_s1 = ev_mm[1]
                        ks, st_ = SIL.next()
                        ev_s = P.op("act", lambda e, st_=st_, b=pa: e.activation(out=st_[:], in_=ps[b][:, :], func=AF.Silu),
                                    waits=[ev_mm[0]] + SIL.free[ks], sig=True)
                        bank_free[pa] = [ev_s]
                        ev_g = P.op("dve", lambda e, st_=st_, b=pb, j=j, n=n: e.tensor_tensor(
                            out=G[:, j, n * 512:(n + 1) * 512], in0=st_[:], in1=ps[b][:, :], op=ALU.mult),
                            waits=[ev_s, ev_mm[1]] + (bg_done if (j == 28 and n == 0) else []), sig=True)
                        SIL.free[ks] = [ev_g]
                        bank_free[pb] = [ev_g]
                        g_evs.append(ev_g)
                    WR.free[k] = [ev_last_s1]
                    bg_tick(j)
                if ht_next:
                    s_hn = P.dsem()
                    ht_ready[0] = [P.dma("sp", HT[:, :, :], HTd, s_hn, waits=[ev_last_s1] + bg_done)]
                store_evs = []
                pi = 0
                groups44 = [(kg, min(8, NFC - kg)) for kg in range(0, NFC, 8)]
                for s in range(4):
                    ev_o = tokmajor_mm(groups44, w_loader(w2v, s * 512), lambda kc, g: G[:, kc, g * 128:(g + 1) * 128],
                                       NFC, 8, first_waits=[g_evs[-1]] if s == 0 else ())
                    for g in range(8):
                        kx, xt_, ev_x = xp_q.pop(0)
                        ev_e = P.op("dve", lambda e, xt_=xt_, g=g: e.scalar_tensor_tensor(
                            out=xt_[:], in0=ps[g][:, :], scalar=0.5, in1=xt_[:], op0=ALU.mult, op1=ALU.add),
                            waits=[ev_o[g], ev_x], sig=True)
                        bank_free[g] = [ev_e]
                        ev_st = P.dma("sp", dst_rows[g * 128:(g + 1) * 128, s * 512:(s + 1) * 512], xt_[:], XP.sem[kx], waits=[ev_e])
                        XP.free[kx] = [ev_st]
                        store_evs.append(ev_st)
                        if pi + 8 < len(pieces):
                            xp_q.append(load_piece(pi + 8))
                        pi += 1
                P.fence()
                P.scope_end()
                return store_evs[-4:]

        def headnorm_epilogue(st):
            pass

        class HeadNorm:
            def __init__(self, ph, N):
                self.N = N
                self.SQ = Ring(P, ph, "hsq", 2, [128, N], BF16)
                self.LN = Ring(P, ph, "hln", 2, [128, N], F32)
                self.RB = Ring(P, ph, "hrb", 2, [128, N], F32)
                self.cnt = 0
                self.prev = None

            def main(self, wsl, rhs_fn, gidx, out_fn, first_waits=()):
                N = self.N
                i = self.cnt
                self.cnt += 1
                b = i % 3
                ev = None
                for kc in range(NKC):
                    wl = (list(first_waits) + bank_free[b]) if kc == 0 else []
                    ev = P.op("pe", lambda e, b=b, kc=kc, wsl=wsl, rhs_fn=rhs_fn: e.matmul(
                        ps[b][:, 0:N], wsl[:, kc, :], rhs_fn(kc), start=(kc == 0), stop=(kc == NKC - 1)),
                        waits=wl, sig=(kc == NKC - 1))
                ks, sq = self.SQ.next()
                ev_sq = P.op("act", lambda e, sq=sq, b=b: e.activation(out=sq[:], in_=ps[b][:, 0:N], func=AF.Square),
                             waits=[ev] + self.SQ.free[ks], sig=True)
                cur = (i, b, ks, sq, ev_sq, gidx, out_fn, ev)
                self.flush()
                self.prev = cur
                return ev

            def flush(self):
                if self.prev is None:
                    return
                N = self.N
                i, b, ks, sq, ev_sq, gidx, out_fn, ev_main = self.prev
                self.prev = None
                bq = 3 + i % 2
                ev_on = P.op("pe", lambda e, bq=bq, sq=sq: e.matmul(ps[bq][:, 0:N], onesdiv, sq[:], start=True, stop=True),
                             waits=[ev_sq] + bank_free[bq], sig=True)
                self.SQ.free[ks] = [ev_on]
                kl, ln = self.LN.next()
                ev_ln = P.op("act", lambda e, ln=ln, bq=bq: e.activation(out=ln[:], in_=ps[bq][:, 0:N], func=AF.Ln, bias=epsT[:, 0:1], scale=1.0),
                             waits=[ev_on] + self.LN.free[kl], sig=True)
                bank_free[bq] = [ev_ln]
                kr, rb = self.RB.next()
                ev_r = P.op("act", lambda e, ln=ln, rb=rb: e.activation(out=rb[:], in_=ln[:], func=AF.Exp, scale=-0.5),
                            waits=[ev_ln] + self.RB.free[kr], sig=True)
                self.LN.free[kl] = [ev_r]
                ev_out = out_fn(ps[b][:, 0:N], gcol[:, gidx:gidx + 1], rb, [ev_r, ev_main])
                self.RB.free[kr] = [ev_out]
                bank_free[b] = [ev_out]

        def phase_mem():
            normpass(mem, 3, 256)
            with ExitStack() as ph:
                P.scope_begin()
                HN = HeadNorm(ph, 256)
                wv = w_mkv.rearrange("(kc p) n -> p kc n", p=128)
                for hp in range(2):
                    k, wt = WR.next()
                    ev_w = None
                    for hh in range(2):
                        h = hp * 2 + hh
                        wsl = wt[:, hh * 2048:(hh + 1) * 2048].rearrange("p (a b) -> p a b", a=16)
                        ev_w = P.dma("pool", wsl, wv[:, :, h * 128:(h + 1) * 128], WR.sem[k], waits=WR.free[k] if hh == 0 else ())
                    evm = None
                    for hh in range(2):
                        h = hp * 2 + hh
                        wsl = wt[:, hh * 2048:(hh + 1) * 2048].rearrange("p (a b) -> p a b", a=16)

                        def out_fn(psap, gc, rb, waits, h=h):
                            return P.op("dve", lambda e: e.scalar_tensor_tensor(
                                out=CK[:, h * 256:(h + 1) * 256], in0=psap, scalar=gc, in1=rb[:], op0=ALU.mult, op1=ALU.mult),
                                waits=waits, sig=True)
                        evm = HN.main(wsl, lambda kc: HT[:, kc, 0:256], 5, out_fn,
                                      first_waits=[ev_w] + ht_ready[0])
                    WR.free[k] = [evm]
                HN.flush()
                ev_o = tokmajor_mm([(0, 8), (8, 8)], w_loader(wv, 512), lambda kc, g: HT[:, kc, g * 128:(g + 1) * 128], NKC, 2)
                for g in range(2):
                    evc = P.op("dve", lambda e, g=g: e.tensor_copy(out=CV[:, g * 512:(g + 1) * 512], in_=ps[g][:, :]),
                               waits=[ev_o[g]], sig=True)
                    bank_free[g] = [evc]
                P.fence()
                P.scope_end()

        def projection(tt):
            col_t = tt * T
            winv = w_in.rearrange("(kc p) n -> p kc n", p=128)
            with ExitStack() as ph:
                P.scope_begin()
                HN = HeadNorm(ph, 512)
                STG = Ring(P, ph, "stg", 3, [128, 512], BF16, dma=True)
                QF = Ring(P, ph, "qf", 2, [128, 512], F32)
                VST = Ring(P, ph, "vst", 4, [128, 512], BF16, dma=True)
                MRS = Ring(P, ph, "mrs", 2, [8, 512], BF16, dma=True)
                gsm = sb("gsm", [128, 3 * 32], F32, ph)
                wff = sb("wff", [128, NKC, 8], BF16, ph)
                eb = sb("eb", [8, 512], F32, ph)
                chunks = []
                for h in range(4):
                    chunks.append((C_BK + h * 128, 3, KTs, 8 + h, h, None))
                for h in range(8):
                    chunks.append((C_FK + h * 128, 1, KTs, h, None, None))
                for h in range(8):
                    chunks.append((C_FQ + h * 128, 0, QTs, h, None, None))
                for h in range(4):
                    chunks.append((C_BQ + h * 128, 2, QTs, 8 + h, None, h))
                for h in range(4):
                    chunks.append((C_CQ + h * 128, 4, QTs, 12 + h, None, None))
                km_last = [None]
                gsm2 = [gsm, sb("gsm_b", [128, 3 * 32], F32, ph)]
                gsm_free = [[], []]
                gcount = [0]
                deferred = []

                def defer(fn, delay):
                    deferred.append([delay, fn])

                def tick(all_=False):
                    while True:
                        snap = list(deferred)
                        deferred.clear()
                        ran = False
                        for d_ in snap:
                            if all_ or d_[0] <= 0:
                                d_[1]()
                                ran = True
                            else:
                                d_[0] -= 1
                                deferred.append(d_)
                        if not (all_ and deferred):
                            break

                def gating(mq_i, n, kq, qf, ev_qf):
                    gi = gcount[0] % 2
                    gcount[0] += 1
                    gs = gsm2[gi]
                    bg, bt = 5, 6

                    def partA():
                        ev_g = None
                        for g in range(4):
                            ev_g = P.op("pe", lambda e, g=g: e.matmul(
                                ps[bg][:, g * 8:(g + 1) * 8], qf[:, g * 128:(g + 1) * 128], kmT[:, mq_i * 8:(mq_i + 1) * 8],
                                start=True, stop=True), waits=([ev_qf, km_last[0]] + bank_free[bg]) if g == 0 else (), sig=(g == 3))
                        QF.free[kq] = [ev_g]
                        ev_m = P.op("dve", lambda e: e.tensor_tensor(
                            out=gs[:, 0:32], in0=ps[bg][:, 0:32], in1=pastm[:, n * 32:(n + 1) * 32], op=ALU.add),
                            waits=[ev_g] + gsm_free[gi], sig=True)
                        bank_free[bg] = [ev_m]
                        ev_n = None
                        for g in range(4):
                            own = 4 + 2 * n + g // 2
                            e1 = P.op("dve", lambda e, g=g: e.max(out=gs[:, 32 + g * 8:40 + g * 8], in_=gs[:, g * 8:(g + 1) * 8]),
                                      waits=[ev_m], sig=True)
                            e2 = P.op("dve", lambda e, g=g: e.tensor_scalar(
                                out=gs[:, 64 + g * 8:72 + g * 8], in0=gs[:, g * 8:(g + 1) * 8],
                                scalar1=gs[:, 32 + g * 8 + 2:32 + g * 8 + 3], scalar2=NEGB, op0=ALU.is_lt, op1=ALU.mult),
                                waits=[e1], sig=True)
                            ev_n = P.op("dve", lambda e, g=g, own=own: e.memset(gs[:, 64 + g * 8 + own:64 + g * 8 + own + 1], 0.0),
                                        waits=[e2], sig=True)

                        def partB():
                            ev_t = None
                            for g in range(4):
                                ev_t = P.op("pe", lambda e, g=g: e.transpose(
                                    ps[bt][0:8, g * 128:(g + 1) * 128], gs[:, 64 + g * 8:72 + g * 8], identf),
                                    waits=([ev_n] + bank_free[bt]) if g == 0 else (), sig=(g == 3))
                            gsm_free[gi] = [ev_t]
                            km, mrs = MRS.next()
                            ev_c = P.op("act", lambda e: e.copy(out=mrs[:], in_=ps[bt][0:8, :]), waits=[ev_t] + MRS.free[km], sig=True)
                            bank_free[bt] = [ev_c]
                            MRS.free[km] = [P.dma("sp", MRs[mq_i, :, n * 512:(n + 1) * 512], mrs[:], MRS.sem[km], waits=[ev_c])]
                        defer(partB, 1)
                    defer(partA, 1)

                for ci in range(0, len(chunks), 2):
                    k, wt = WR.next()
                    ev_w = None
                    for hh in range(2):
                        co = chunks[ci + hh][0]
                        wsl = wt[:, hh * 2048:(hh + 1) * 2048].rearrange("p (a b) -> p a b", a=16)
                        ev_w = P.dma("pool", wsl, winv[:, :, co:co + 128], WR.sem[k], waits=WR.free[k] if hh == 0 else ())
                    evm = None
                    for hh in range(2):
                        co, gidx, dscr_, dh, mk_i, mq_i = chunks[ci + hh]
                        wsl = wt[:, hh * 2048:(hh + 1) * 2048].rearrange("p (a b) -> p a b", a=16)
                        for n in range(2):
                            def out_fn(psap, gc, rb, waits, dscr_=dscr_, dh=dh, n=n, mk_i=mk_i, mq_i=mq_i):
                                ksg, stg = STG.next()
                                ev_q = P.op("dve", lambda e: e.scalar_tensor_tensor(
                                    out=stg[:], in0=psap, scalar=gc, in1=rb[:], op0=ALU.mult, op1=ALU.mult),
                                    waits=waits + STG.free[ksg], sig=True)
                                ev_last = ev_q
                                if mk_i is not None:
                                    blk = (col_t + n * 512) // 256
                                    ev_last = P.op("dve", lambda e: e.tensor_reduce(
                                        out=kmT[:, mk_i * 8 + blk:mk_i * 8 + blk + 2],
                                        in_=stg[:].rearrange("p (a b) -> p a b", a=2), axis=AX.X, op=ALU.add),
                                        waits=[ev_q], sig=True)
                                    km_last[0] = ev_last
                                ev_st = P.dma("sp", dscr_[dh, :, col_t + n * 512:col_t + (n + 1) * 512], stg[:], STG.sem[ksg], waits=[ev_q])
                                STG.free[ksg] = [ev_st]
                                if mq_i is not None and tt == 1:
                                    kq, qf = QF.next()
                                    ev_last = P.op("dve", lambda e: e.scalar_tensor_tensor(
                                        out=qf[:], in0=psap, scalar=gc, in1=rb[:], op0=ALU.mult, op1=ALU.mult),
                                        waits=QF.free[kq], sig=True)
                                    gating(mq_i, n, kq, qf, ev_last)
                                return ev_last
                            evm = HN.main(wsl, lambda kc, n=n: HT[:, kc, n * 512:(n + 1) * 512], gidx, out_fn,
                                          first_waits=([ev_w] + ht_ready[0]) if (hh == 0 and n == 0) else ())
                            tick()
                    WR.free[k] = [evm]
                HN.flush()
                tick(all_=True)
                for vs, co in enumerate((C_FV, C_FV + 512, C_BV)):
                    ev_o = tokmajor_mm([(0, 8), (8, 8)], w_loader(winv, co), lambda kc, g: HT[:, kc, g * 128:(g + 1) * 128], NKC, 8)
                    for g in range(8):
                        kv, vst = VST.next()
                        if g % 2:
                            ev_c = P.op("act", lambda e, vst=vst, g=g: e.copy(out=vst[:], in_=ps[g][:, :]), waits=[ev_o[g]] + VST.free[kv], sig=True)
                        else:
                            ev_c = P.op("dve", lambda e, vst=vst, g=g: e.tensor_copy(out=vst[:], in_=ps[g][:, :]), waits=[ev_o[g]] + VST.free[kv], sig=True)
                        bank_free[g] = [ev_c]
                        VST.free[kv] = [P.dma("sp", Vs[col_t + g * 128:col_t + (g + 1) * 128, vs * 512:(vs + 1) * 512], vst[:], VST.sem[kv], waits=[ev_c])]
                ev_w = P.dma("pool", wff[:], winv[:, :, C_FF:C_FF + 8], s_ff, waits=list(P.last_fence))
                ev_sp = None
                for n in range(2):
                    b = n
                    ev = None
                    for kc in range(NKC):
                        wl = ([ev_w] + bank_free[b]) if kc == 0 else []
                        ev = P.op("pe", lambda e, b=b, kc=kc, n=n: e.matmul(
                            ps[b][0:8, :], wff[:, kc, :], HT[:, kc, n * 512:(n + 1) * 512],
                            start=(kc == 0), stop=(kc == NKC - 1)), waits=wl, sig=(kc == NKC - 1))
                    ev_e = P.op("act", lambda e, b=b: e.activation(out=eb[:], in_=ps[b][0:8, :], func=AF.Exp, bias=negb[:, 0:1], scale=-1.0),
                                waits=[ev, ev_sp], sig=True)
                    bank_free[b] = [ev_e]
                    ev_sp = P.op("act", lambda e, n=n: e.activation(
                        out=SPT[:, col_t + n * 512:col_t + (n + 1) * 512], in_=eb[:], func=AF.Ln, bias=oneT[0:8, 0:1], scale=1.0),
                        waits=[ev_e], sig=True)
                P.fence()
                P.scope_end()

        def cprep_body(ph):
            if True:
                onesr = sb("onesr", [8, S], F32, ph)
                Cp = sb("Cp", [8, S], F32, ph)
                ctp = sb("ctp", [8, S], F32, ph)
                r1 = sb("r1", [8, S], F32, ph)
                hi = sb("hi", [8, S], BF16, ph)
                mid = sb("mid", [8, S], BF16, ph)
                lo = sb("lo", [8, S], BF16, ph)
                sc = P.dsem()
                e0 = P.op("dve", lambda e: e.memset(onesr[:], 1.0), sig=True)
                e1 = P.op("dve", lambda e: e.tensor_tensor_scan(out=Cp[:], data0=onesr[:], data1=SPT[:], initial=0.0, op0=ALU.mult, op1=ALU.add),
                          waits=[e0], sig=True)
                e2 = P.op("dve", lambda e: e.tensor_scalar(out=ctp[:], in0=Cp[:], scalar1=-RS, scalar2=None, op0=ALU.mult), waits=[e1], sig=True)
                e3 = P.op("dve", lambda e: e.tensor_copy(out=hi[:], in_=ctp[:]), waits=[e2], sig=True)
                e4 = P.op("dve", lambda e: e.tensor_tensor(out=r1[:], in0=ctp[:], in1=hi[:], op=ALU.subtract), waits=[e3], sig=True)
                e5 = P.op("dve", lambda e: e.tensor_copy(out=mid[:], in_=r1[:]), waits=[e4], sig=True)
                e6 = P.op("dve", lambda e: e.tensor_tensor(out=ctp[:], in0=r1[:], in1=mid[:], op=ALU.subtract), waits=[e5], sig=True)
                e7 = P.op("dve", lambda e: e.tensor_copy(out=lo[:], in_=ctp[:]), waits=[e6], sig=True)
                P.dma("sp", cs3[0], hi[:], sc, waits=[e3])
                P.dma("sp", cs3[1], mid[:], sc, waits=[e5])
                P.dma("sp", cs3[2], lo[:], sc, waits=[e7])
                res = {"st": (sc, sc.count), "cpc": None}

                def part2():
                    ev = None
                    for g in range(16):
                        ev = P.op("pe", lambda e, g=g: e.transpose(ps[0][:, g * 8:(g + 1) * 8], Cp[:, g * 128:(g + 1) * 128], identf[0:8, 0:8]),
                                  waits=([e1] + bank_free[0]) if g == 0 else (), sig=(g == 15))
                    evc = P.op("dve", lambda e: e.tensor_copy(out=CpC[:], in_=ps[0][:, 0:128]), waits=[ev], sig=True)
                    bank_free[0] = [evc]
                    res["cpc"] = evc
                res["part2"] = part2
                return res

        def attention():
            with ExitStack() as ph:
                P.scope_begin()
                QR = Ring(P, ph, "aq", 2, [128, S], BF16, dma=True)
                KR = Ring(P, ph, "ak", 2, [128, S], BF16, dma=True)
                VR = Ring(P, ph, "av", 2, [128, 16, 128], BF16, dma=True)
                CR = Ring(P, ph, "ac", 2, [128, S], BF16, dma=True)
                MR = Ring(P, ph, "am", 2, [128, 1024], BF16, dma=True)
                for ring in (CR, MR):
                    for i_, t_ in enumerate(ring.t):
                        ring.free[i_] = [P.op("dve", lambda e, t_=t_: e.memset(t_[:], 0.0), sig=True)]
                PT = Ring(P, ph, "pt", 6, [128, 512], BF16)
                RL = Ring(P, ph, "rl", 2, [128, 512], F32)
                OS = Ring(P, ph, "os", 2, [128, 512], BF16, dma=True)
                ACC = Ring(P, ph, "acc", 2, [128, 512], F32)
                ACD = Ring(P, ph, "acd", 2, [128, 512], F32)
                ACB = Ring(P, ph, "acb", 2, [128, 512], BF16)
                s_ot = P.dsem()
                zt = sb("zt", [128, 512], BF16, ph)
                ev_zt = P.op("dve", lambda e: e.memset(zt[:], 0.0), sig=True)
                Vsv = Vs.rearrange("(g p) c -> p g c", p=128)
                Gm = sb("Gm", [128, 4 * GW], BF16, ph)
                s_gm = P.dsem()
                ev_gm = None
                for h_ in range(4):
                    ev_gm = P.dma("sp", Gm[:, h_ * GW:(h_ + 1) * GW], bass.AP(gv2d.tensor, h_ * GV, [[1, 128], [1, GW]]), s_gm)

                def kind_of(h):
                    return "fox" if h < 8 else ("moba" if h < 12 else "mem")

                def loads(h):
                    kind = kind_of(h)
                    d = {}
                    kq, qt = QR.next()
                    d["q"] = (kq, qt, P.dma("sp", qt[:], QTs[h], QR.sem[kq], waits=QR.free[kq]))
                    if kind != "mem":
                        kk, kt = KR.next()
                        d["k"] = (kk, kt, P.dma("sp", kt[:], KTs[h], KR.sem[kk], waits=KR.free[kk]))
                        kv, vt = VR.next()
                        d["v"] = (kv, vt, P.dma("sp", vt[:], Vsv[:, :, h * 128:(h + 1) * 128], VR.sem[kv], waits=VR.free[kv]))
                    if kind == "fox":
                        kc_, ct = CR.next()
                        d["c"] = (kc_, ct, P.dma("sp", ct[0:3, :], cs3[:, h, :], CR.sem[kc_], waits=CR.free[kc_] + [cp_evs["st"]]))
                    if kind == "moba":
                        km, mt = MR.next()
                        d["m"] = (km, mt, P.dma("sp", mt[0:8, :], MRs[h - 8], MR.sem[km], waits=MR.free[km]))
                    return d

                tcount = [0]

                def head(h, d):
                    kind = kind_of(h)
                    qt = d["q"][1]
                    ld_evs = [v[2] for v in d.values()] + [ev_gm] + ([cp_evs["cpc"]] if kind == "fox" else [])
                    tiles = []
                    for I in range(4):
                        nk = 2 if kind == "mem" else 4 * I + 4
                        for j in range(nk):
                            tiles.append((I, j, nk))
                    ev_s = {}
                    ev_last_pe = [None]
                    qs = {}
                    st_evs = {}
                    pendingL = []

                    def tickL(all_=False):
                        for it in list(pendingL):
                            if all_ or it[0] <= 0:
                                pendingL.remove(it)
                                it[1]()
                            else:
                                it[0] -= 1

                    def c0_of(I, j):
                        if kind == "mem":
                            return 0
                        return max(0, 128 * (j - 4 * I))

                    def emit_S(idx):
                        I, j, nk = tiles[idx]
                        b = (0, 1, 2, 7)[tcount[0] % 4]
                        tcount[0] += 1
                        c0 = c0_of(I, j)
                        if kind == "mem":
                            lhs = CK[:, (h - 12) * 256 + j * 128:(h - 12) * 256 + (j + 1) * 128]
                        else:
                            lhs = d["k"][1][:, j * 128:(j + 1) * 128]
                        rhs = qt[:, I * 512 + c0:(I + 1) * 512]
                        outp = ps[b][:, c0:512]
                        extra = []
                        if kind == "fox":
                            ct = d["c"][1]
                            extra.append((ones3z, ct[:, I * 512 + c0:(I + 1) * 512]))
                            if j >= 4 * I:
                                off = 384 - 128 * (j - 4 * I)
                                extra.append((antib, mstrip[:, off + c0:off + 512]))
                        elif kind == "moba":
                            off = min(512 * I - 128 * j + 384, 640)
                            extra.append((antib, Gm[:, (h - 8) * GW + off + c0:(h - 8) * GW + off + 512]))
                            if I >= 2:
                                J = j // 2
                                mt = d["m"][1]
                                extra.append((esel[:, J * 128:(J + 1) * 128], mt[:, (I - 2) * 512 + c0:(I - 1) * 512]))
                        wl = bank_free[b] + (ld_evs if idx == 0 else [])
                        ev = P.op("pe", lambda e: e.matmul(outp, lhs, rhs, start=True, stop=(len(extra) == 0)),
                                  waits=wl, sig=(len(extra) == 0))
                        for xi, (l2, r2) in enumerate(extra):
                            last = xi == len(extra) - 1
                            ev = P.op("pe", lambda e, l2=l2, r2=r2, last=last: e.matmul(outp, l2, r2, start=False, stop=last),
                                      sig=last)
                        ev_s[idx] = (b, ev)

                    def emit_PV(idx):
                        I, j, nk = tiles[idx]
                        b, ev = ev_s[idx]
                        c0 = c0_of(I, j)
                        kp, pt = PT.next()
                        if kind == "fox":
                            bias = CpC[:, j * 8 + h:j * 8 + h + 1]
                            ev_x = P.op("act", lambda e: e.activation(out=pt[:, c0:512], in_=ps[b][:, c0:512], func=AF.Exp, bias=bias, scale=SCALE),
                                        waits=[ev] + PT.free[kp], sig=True)
                        else:
                            ev_x = P.op("act", lambda e: e.activation(out=pt[:, c0:512], in_=ps[b][:, c0:512], func=AF.Exp, scale=SCALE),
                                        waits=[ev] + PT.free[kp], sig=True)
                        bank_free[b] = [ev_x]
                        bo, bl = 3 + I % 2, 5 + I % 2
                        if kind == "mem":
                            vl = CV[:, j * 512 + (h - 12) * 128:j * 512 + (h - 12 + 1) * 128]
                        else:
                            vl = d["v"][1][:, j, :]
                        ev_p = P.op("pe", lambda e: e.matmul(ps[bo][:, c0:512], vl, pt[:, c0:512], start=(j == 0), stop=(j == nk - 1)),
                                    waits=[ev_x] + (bank_free[bo] if j == 0 else []), sig=True)
                        ev_last_pe[0] = ev_p
                        if j == 0:
                            qs["kb"], qs["acb"] = ACB.next()
                            qs["ka"], qs["accP"] = ACC.next()
                            _, qs["accD"] = ACD.next()
                            qs["evP"] = None
                            qs["evD"] = None
                            qs["cD"] = None
                            qs["ev"] = None
                        acb, accP, accD = qs["acb"], qs["accP"], qs["accD"]
                        npe = 3 if nk >= 8 else nk
                        if j < nk - npe:
                            if j % 3 == 0:
                                if j == 0:
                                    qs["evP"] = P.op("pool", lambda e: e.tensor_tensor(out=accP[:], in0=pt[:], in1=zt[:], op=ALU.add),
                                                     waits=[ev_x, ev_zt] + ACC.free[qs["ka"]], sig=True)
                                else:
                                    qs["evP"] = P.op("pool", lambda e: e.tensor_tensor(out=accP[:, c0:512], in0=accP[:, c0:512], in1=pt[:, c0:512], op=ALU.add),
                                                     waits=[ev_x, qs["evP"]], sig=True)
                                ev_acc = qs["evP"]
                            else:
                                if qs["cD"] is None:
                                    qs["cD"] = c0
                                    qs["evD"] = P.op("dve", lambda e: e.tensor_tensor(out=accD[:, c0:512], in0=pt[:, c0:512], in1=zt[:, c0:512], op=ALU.add),
                                                     waits=[ev_x], sig=True)
                                else:
                                    qs["evD"] = P.op("dve", lambda e: e.tensor_tensor(out=accD[:, c0:512], in0=accD[:, c0:512], in1=pt[:, c0:512], op=ALU.add),
                                                     waits=[ev_x, qs["evD"]], sig=True)
                                ev_acc = qs["evD"]
                            PT.free[kp] = [ev_p, ev_acc]
                            if j == nk - npe - 1:
                                cD = qs["cD"]
                                wl = [qs["evP"], qs["evD"]] + ACB.free[qs["kb"]]
                                if cD is None:
                                    ev_cb = P.op("dve", lambda e: e.tensor_copy(out=acb[:], in_=accP[:]), waits=wl, sig=True)
                                else:
                                    if cD > 0:
                                        P.op("dve", lambda e: e.tensor_copy(out=acb[:, 0:cD], in_=accP[:, 0:cD]), waits=wl)
                                        wl = []
                                    ev_cb = P.op("dve", lambda e: e.tensor_tensor(out=acb[:, cD:512], in0=accP[:, cD:512], in1=accD[:, cD:512], op=ALU.add),
                                                 waits=wl, sig=True)
                                qs["ev"] = ev_cb
                                ACC.free[qs["ka"]] = [ev_cb]
                        else:
                            first = (j == nk - npe)
                            has_acc = npe < nk
                            ev_l = P.op("pe", lambda e: e.matmul(ps[bl][:, c0:512], onesb, pt[:, c0:512], start=first,
                                                                 stop=(j == nk - 1 and not has_acc)),
                                        waits=bank_free[bl] if first else (), sig=True)
                            PT.free[kp] = [ev_l]
                            if j == nk - 1 and has_acc:
                                ev_l = P.op("pe", lambda e: e.matmul(ps[bl][:, :], onesb, acb[:], start=False, stop=True),
                                            waits=[qs["ev"]], sig=True)
                                ACB.free[qs["kb"]] = [ev_l]
                            ev_last_pe[0] = ev_l
                        if j == nk - 1:

                            def finish():
                                kr, rl = RL.next()
                                ev_r0 = P.op("act", lambda e: e.activation(out=rl[:], in_=ps[bl][:, :], func=AF.Ln), waits=[ev_l] + RL.free[kr], sig=True)
                                bank_free[bl] = [ev_r0]
                                ev_r = P.op("act", lambda e: e.activation(out=rl[:], in_=rl[:], func=AF.Exp, scale=-1.0), waits=[ev_r0], sig=True)
                                ko, ot = OS.next()
                                ev_o = P.op("dve", lambda e: e.tensor_tensor(out=ot[:], in0=ps[bo][:, :], in1=rl[:], op=ALU.mult),
                                            waits=[ev_r, ev_p] + OS.free[ko], sig=True)
                                bank_free[bo] = [ev_o]
                                RL.free[kr] = [ev_o]
                                ev_st = P.dma("sp", OTs[h, :, I * 512:(I + 1) * 512], ot[:], OS.sem[ko], waits=[ev_o])
                                OS.free[ko] = [ev_st]
                                st_evs[I] = ev_st

                            pendingL.append([1 if kind == "mem" else 2, finish])

                    for i0 in range(min(3, len(tiles))):
                        emit_S(i0)
                    for idx in range(len(tiles)):
                        if idx + 3 < len(tiles):
                            emit_S(idx + 3)
                        emit_PV(idx)
                        tickL()
                    tickL(all_=True)
                    for key, ring in (("q", QR), ("k", KR), ("v", VR), ("c", CR), ("m", MR)):
                        if key in d:
                            ring.free[d[key][0]] = [ev_last_pe[0]]
                    P.dma("sp", HT[:, h, :], OTs[h, :, 0:T], s_ot, waits=[st_evs[0], st_evs[1]])

                horder = [12, 13, 14, 15, 8, 9, 10, 11, 0, 1, 2, 3, 4, 5, 6, 7]
                nxt = loads(horder[0])
                cp_evs = cprep_body(ph)
                for hi_, h in enumerate(horder):
                    if h == 0:
                        cp_evs["part2"]()
                    cur = nxt
                    if hi_ + 1 < 16:
                        nxt = loads(horder[hi_ + 1])
                    head(h, cur)
                P.fence()
                P.scope_end()

        def outproj_both():
            wov = w_out.rearrange("(kc p) n -> p kc n", p=128)
            with ExitStack() as ph:
                P.scope_begin()
                XP = Ring(P, ph, "xq", 16, [128, 512], F32, dma=True)
                HT2 = sb("HT2", [128, NKC, T], BF16, ph)
                so = P.dsem()
                ev_ot = P.dma("sp", HT2[:, :, :], OTs.rearrange("h p t -> p h t")[:, :, T:2 * T], so)
                for tt in range(NT):
                    rows = slice(tt * T, (tt + 1) * T)
                    src = HT if tt == 0 else HT2
                    res_rows = x1s[rows, :]
                    dst_rows = x2s[rows, :]
                    for s in range(4):
                        xl = []
                        for g in range(8):
                            kx, xt_ = XP.next()
                            xl.append((kx, xt_, P.dma("sp", xt_[:], res_rows[g * 128:(g + 1) * 128, s * 512:(s + 1) * 512], XP.sem[kx], waits=XP.free[kx])))
                        ev_o = tokmajor_mm([(0, 8), (8, 8)], w_loader(wov, s * 512), lambda kc, g, src=src: src[:, kc, g * 128:(g + 1) * 128], NKC, 8,
                                           first_waits=[ev_ot] if (s == 0 and tt == 1) else ())
                        for g in range(8):
                            kx, xt_, ev_x = xl[g]
                            ev_e = P.op("dve", lambda e, xt_=xt_, g=g: e.tensor_tensor(out=xt_[:], in0=ps[g][:, :], in1=xt_[:], op=ALU.add),
                                        waits=[ev_o[g], ev_x], sig=True)
                            bank_free[g] = [ev_e]
                            XP.free[kx] = [P.dma("sp", dst_rows[g * 128:(g + 1) * 128, s * 512:(s + 1) * 512], xt_[:], XP.sem[kx], waits=[ev_e])]
                P.fence()
                P.scope_end()

        r0, r1 = slice(0, T), slice(T, 2 * T)
        phase_mem()
        normpass(x[r0, :], 0, T)
        ffn(w_ffn[0], x[r0, :], x1s[r0, :], bg=(x[r1, :], 0), ht_next=True)
        ffn(w_ffn[0], x[r1, :], x1s[r1, :], bg=(x1s[r0, :], 1), ht_next=True)
        projection(0)
        normpass(x1s[r1, :], 1, T)
        projection(1)
        attention()
        outproj_both()
        normpass(x2s[r0, :], 2, T)
        ffn(w_ffn[1], x2s[r0, :], out[r0, :], bg=(x2s[r1, :], 2), ht_next=True)
        ffn(w_ffn[1], x2s[r1, :], out[r1, :])
        P.fence()
        block = es.enter_context(nc.Block())
        P.replay(block)
    return nc


_CONSTS = None


def make_in_maps(inputs, n_cores=8):
    global _CONSTS
    if _CONSTS is None:
        _CONSTS = host_consts()
    cb16, cf32, esel, oh, negv = _CONSTS
    f = lambda a: np.ascontiguousarray(np.asarray(a, dtype=np.float32))
    norms = np.stack([f(inputs["ffn1_norm"])[0], f(inputs["mix_norm"])[0], f(inputs["ffn2_norm"])[0], f(inputs["mem_norm"])[0]], axis=0)
    normsT = norms.reshape(4, 16, 128).transpose(2, 0, 1).reshape(128, 64)
    gains = np.stack([f(inputs[k])[0] for k in ("fox_q_gain", "fox_k_gain", "moba_q_gain", "moba_k_gain", "mem_q_gain", "mem_k_gain")], axis=1)
    shared = {
        "ffn1_w1": f(inputs["ffn1_w1"])[0], "ffn1_w3": f(inputs["ffn1_w3"])[0], "ffn1_w2": f(inputs["ffn1_w2"])[0],
        "ffn2_w1": f(inputs["ffn2_w1"])[0], "ffn2_w3": f(inputs["ffn2_w3"])[0], "ffn2_w2": f(inputs["ffn2_w2"])[0],
        "w_in": f(inputs["w_in"])[0], "w_mem_kv": f(inputs["w_mem_kv"])[0], "w_out": f(inputs["w_out"])[0],
        "normsT": np.ascontiguousarray(normsT), "gains": np.ascontiguousarray(gains),
        "b_forget": f(inputs["b_forget"]).reshape(8, 1), "rel_bias": f(inputs["rel_bias"]),
        "cb16": cb16, "cf32": cf32, "esel": esel, "oh": oh, "negv": negv,
    }
    xs = f(inputs["x"])
    ms = f(inputs["mem"])
    maps = []
    for c in range(n_cores):
        m = dict(shared)
        m["x"] = xs[c]
        m["mem"] = ms[c]
        maps.append(m)
    return maps


_NC = None


def kernel(**inputs):
    global _NC
    if _NC is None:
        _NC = build()
    maps = make_in_maps(inputs, 8)
    res = run_bass_kernel_spmd(_NC, maps, core_ids=list(range(8)))
    return np.stack([np.asarray(r["out"], dtype=np.float32) for r in res.results], axis=0)
```
